# Optimizing a Trainium2 kernel written in Bass

```python
import math
import jax, jax.numpy as jnp
from jax import lax
import numpy as np

D_MODEL = 1024
BATCH = 4
SEQ = 4096
DEPTH = 2

N_MIXERS = 2
GRID_W = 64
Q_BLOCK = 128
EPS = 1e-6

A_HEAD_DIM = 64
A_N_HEADS = D_MODEL // (2 * A_HEAD_DIM)
A_ROT = A_HEAD_DIM // 4
ROPE_THETA_1D = 500000.0

B_HEAD_DIM = 128
B_N_HEADS = D_MODEL // B_HEAD_DIM
B_N_KV = max(1, B_N_HEADS // 4)
B_GROUP = B_N_HEADS // B_N_KV
ROPE_THETA_AXIAL = 10000.0

D_FF = -(-8 * D_MODEL // (3 * 256)) * 256

N_A_LAYERS = (DEPTH + 1) // 2
N_B_LAYERS = DEPTH // 2

kernel_name = "hybrid_diffattn_axialgqa_adaln_encoder"


def rmsnorm(x, g):
    xf = x.astype(jnp.float32)
    y = xf * lax.rsqrt(jnp.mean(xf * xf, axis=-1, keepdims=True) + EPS)
    return (y * g.astype(jnp.float32)).astype(x.dtype)


def rope_angles(pos, dim, theta):
    half = dim // 2
    freqs = theta ** (-jnp.arange(half, dtype=jnp.float32) / half)
    ang = pos.astype(jnp.float32)[:, None] * freqs[None, :]
    return jnp.cos(ang), jnp.sin(ang)


def apply_rope(x, cos, sin):
    shape = (1, x.shape[1]) + (1,) * (x.ndim - 3) + (cos.shape[-1],)
    c = cos.reshape(shape).astype(x.dtype)
    s = sin.reshape(shape).astype(x.dtype)
    x1, x2 = jnp.split(x, 2, axis=-1)
    return jnp.concatenate([x1 * c - x2 * s, x2 * c + x1 * s], axis=-1)


def lambda_init_fn(layer_idx):
    return 0.8 - 0.6 * math.exp(-0.3 * layer_idx)


def diff_attention(h, w_qkv, w_o, lq1, lk1, lq2, lk2, subln_g, lam_init, cos, sin):
    B_, S, _ = h.shape
    qkv = h @ w_qkv
    q, k, v = jnp.split(qkv, 3, axis=-1)
    q = q.reshape(B_, S, A_N_HEADS, 2, A_HEAD_DIM)
    k = k.reshape(B_, S, A_N_HEADS, 2, A_HEAD_DIM)
    v = v.reshape(B_, S, A_N_HEADS, 2 * A_HEAD_DIM)
    q = jnp.concatenate([apply_rope(q[..., :A_ROT], cos, sin), q[..., A_ROT:]], axis=-1)
    k = jnp.concatenate([apply_rope(k[..., :A_ROT], cos, sin), k[..., A_ROT:]], axis=-1)
    q = q * (A_HEAD_DIM ** -0.5)
    lam = (jnp.exp(jnp.sum(lq1.astype(jnp.float32) * lk1.astype(jnp.float32)))
           - jnp.exp(jnp.sum(lq2.astype(jnp.float32) * lk2.astype(jnp.float32)))
           + lam_init)
    nb = S // Q_BLOCK
    qb = q.reshape(B_, nb, Q_BLOCK, A_N_HEADS, 2, A_HEAD_DIM).transpose(1, 0, 2, 3, 4, 5)

    def block(qi):
        s = jnp.einsum('bqhcd,bkhcd->bhcqk', qi, k).astype(jnp.float32)
        p = jax.nn.softmax(s, axis=-1)
        w = (p[:, :, 0] - lam * p[:, :, 1]).astype(v.dtype)
        return jnp.einsum('bhqk,bkhe->bqhe', w, v)

    o = lax.map(block, qb)
    o = o.transpose(1, 0, 2, 3, 4).reshape(B_, S, A_N_HEADS, 2 * A_HEAD_DIM)
    o = rmsnorm(o, subln_g) * (1.0 - lam_init)
    return o.reshape(B_, S, A_N_HEADS * 2 * A_HEAD_DIM) @ w_o


def axial_gqa(h, w_qkv, w_o, qnorm_g, knorm_g, cos_r, sin_r, cos_c, sin_c):
    B_, S, _ = h.shape
    qkv = h @ w_qkv
    nq = B_N_HEADS * B_HEAD_DIM
    nkv = B_N_KV * B_HEAD_DIM
    q = qkv[..., :nq].reshape(B_, S, B_N_HEADS, B_HEAD_DIM)
    k = qkv[..., nq:nq + nkv].reshape(B_, S, B_N_KV, B_HEAD_DIM)
    v = qkv[..., nq + nkv:].reshape(B_, S, B_N_KV, B_HEAD_DIM)
    q = rmsnorm(q, qnorm_g)
    k = rmsnorm(k, knorm_g)
    half = B_HEAD_DIM // 2
    q = jnp.concatenate([apply_rope(q[..., :half], cos_r, sin_r), apply_rope(q[..., half:], cos_c, sin_c)], axis=-1)
    k = jnp.concatenate([apply_rope(k[..., :half], cos_r, sin_r), apply_rope(k[..., half:], cos_c, sin_c)], axis=-1)
    q = (q * (B_HEAD_DIM ** -0.5)).reshape(B_, S, B_N_KV, B_GROUP, B_HEAD_DIM)
    nb = S // Q_BLOCK
    qb = q.reshape(B_, nb, Q_BLOCK, B_N_KV, B_GROUP, B_HEAD_DIM).transpose(1, 0, 2, 3, 4, 5)

    def block(qi):
        s = jnp.einsum('bqhgd,bkhd->bhgqk', qi, k).astype(jnp.float32)
        p = jax.nn.softmax(s, axis=-1).astype(v.dtype)
        return jnp.einsum('bhgqk,bkhd->bqhgd', p, v)

    o = lax.map(block, qb)
    o = o.transpose(1, 0, 2, 3, 4, 5).reshape(B_, S, nq)
    return o @ w_o


def swiglu(h, w_in, w_out):
    gate, up = jnp.split(h @ w_in, 2, axis=-1)
    return (jax.nn.silu(gate) * up) @ w_out


def setup_inputs(seed: int = 0) -> dict:
    key = jax.random.key(seed)
    ks = jax.random.split(key, 24)
    f32 = jnp.float32
    D = D_MODEL
    nrm = lambda k, shape, s: jax.random.normal(k, shape, f32) * s
    b_qkv_out = B_N_HEADS * B_HEAD_DIM + 2 * B_N_KV * B_HEAD_DIM
    return {
        "x": nrm(ks[0], (BATCH, SEQ, D), 1.0),
        "c": nrm(ks[1], (BATCH, D), 1.0),
        "ada_w": nrm(ks[2], (DEPTH, D, 6 * D), 0.5 * D ** -0.5),
        "ada_b": nrm(ks[3], (DEPTH, 6 * D), 0.01),
        "norm1_g": 1.0 + nrm(ks[4], (DEPTH, D), 0.02),
        "norm2_g": 1.0 + nrm(ks[5], (DEPTH, D), 0.02),
        "a_w_qkv": nrm(ks[6], (N_A_LAYERS, D, 3 * D), D ** -0.5),
        "a_w_o": nrm(ks[7], (N_A_LAYERS, D, D), D ** -0.5),
        "a_lam_q1": nrm(ks[8], (N_A_LAYERS, A_HEAD_DIM), 0.1),
        "a_lam_k1": nrm(ks[9], (N_A_LAYERS, A_HEAD_DIM), 0.1),
        "a_lam_q2": nrm(ks[10], (N_A_LAYERS, A_HEAD_DIM), 0.1),
        "a_lam_k2": nrm(ks[11], (N_A_LAYERS, A_HEAD_DIM), 0.1),
        "a_subln_g": 1.0 + nrm(ks[12], (N_A_LAYERS, 2 * A_HEAD_DIM), 0.02),
        "b_w_qkv": nrm(ks[13], (N_B_LAYERS, D, b_qkv_out), D ** -0.5),
        "b_w_o": nrm(ks[14], (N_B_LAYERS, B_N_HEADS * B_HEAD_DIM, D), (B_N_HEADS * B_HEAD_DIM) ** -0.5),
        "b_qnorm_g": 1.0 + nrm(ks[15], (N_B_LAYERS, B_HEAD_DIM), 0.02),
        "b_knorm_g": 1.0 + nrm(ks[16], (N_B_LAYERS, B_HEAD_DIM), 0.02),
        "f_w_in": nrm(ks[17], (DEPTH, D, 2 * D_FF), D ** -0.5),
        "f_w_out": nrm(ks[18], (DEPTH, D_FF, D), D_FF ** -0.5),
        "final_g": 1.0 + nrm(ks[19], (D,), 0.02),
    }


def reference(x, c, ada_w, ada_b, norm1_g, norm2_g, a_w_qkv, a_w_o, a_lam_q1, a_lam_k1,
              a_lam_q2, a_lam_k2, a_subln_g, b_w_qkv, b_w_o, b_qnorm_g, b_knorm_g,
              f_w_in, f_w_out, final_g):
    S = x.shape[1]
    rows = S // GRID_W
    t = jnp.arange(S, dtype=jnp.int32)
    row_pos = jnp.broadcast_to(jnp.arange(rows, dtype=jnp.int32)[:, None], (rows, GRID_W)).reshape(S)
    col_pos = jnp.broadcast_to(jnp.arange(GRID_W, dtype=jnp.int32)[None, :], (rows, GRID_W)).reshape(S)
    cos_a, sin_a = rope_angles(t, A_ROT, ROPE_THETA_1D)
    cos_r, sin_r = rope_angles(row_pos, B_HEAD_DIM // 2, ROPE_THETA_AXIAL)
    cos_c, sin_c = rope_angles(col_pos, B_HEAD_DIM // 2, ROPE_THETA_AXIAL)
    cond = jax.nn.silu(c)

    for i in range(DEPTH):
        mod = (cond @ ada_w[i] + ada_b[i])[:, None, :]
        shift1, scale1, gate1, shift2, scale2, gate2 = jnp.split(mod, 6, axis=-1)
        h = rmsnorm(x, norm1_g[i]) * (1.0 + scale1) + shift1
        j = i // N_MIXERS
        if i % N_MIXERS == 0:
            y = diff_attention(h, a_w_qkv[j], a_w_o[j], a_lam_q1[j], a_lam_k1[j],
                               a_lam_q2[j], a_lam_k2[j], a_subln_g[j],
                               lambda_init_fn(i), cos_a, sin_a)
        else:
            y = axial_gqa(h, b_w_qkv[j], b_w_o[j], b_qnorm_g[j], b_knorm_g[j],
                          cos_r, sin_r, cos_c, sin_c)
        x = x + gate1 * y
        h = rmsnorm(x, norm2_g[i]) * (1.0 + scale2) + shift2
        x = x + gate2 * swiglu(h, f_w_in[i], f_w_out[i])

    return rmsnorm(x, final_g)
```

```python
import math
from contextlib import ExitStack

import numpy as np
import ml_dtypes
import concourse.bass as bass
import concourse.mybir as mybir
from concourse.bass_utils import run_bass_kernel_spmd

F32 = mybir.dt.float32
BF16 = mybir.dt.bfloat16
AF = mybir.ActivationFunctionType
ALU = mybir.AluOpType
AX = mybir.AxisListType

D = 1024
SEQ = 4096
NB = 4
NCORES = 8
TOWN = 2048
NT = 16
DFF = 2816
NJ = 22
NG = 11
EPS = 1e-6
VW = 130
KVW = 16 * VW
DBG = {"lvl": 99, "nt": NT}
SAME_SYNC = True


def lambda_init_fn(layer_idx):
    return 0.8 - 0.6 * math.exp(-0.3 * layer_idx)


class Ctx:
    def __init__(self, nc, es):
        self.nc = nc
        self.es = es
        self.nsem = 0
        self.slots = []

    def new_sem(self, name):
        self.nsem += 1
        return self.es.enter_context(self.nc.semaphore(name))


class Engine:
    def __init__(self, ctx, name, h, counts=True):
        self.h = h
        self.name = name
        self.chain = name in ("act", "dve", "pool")
        self.sem = ctx.new_sem("e_" + name) if counts else None
        self.cnt = 0
        self.seen = {}

    def wait(self, tok):
        if tok is None:
            return
        sem, val, key = tok
        if val <= 0:
            return
        if key == self.name and (self.name == "pe" or not SAME_SYNC):
            return
        if self.seen.get(key, 0) >= val:
            return
        self.seen[key] = val
        self.h.wait_ge(sem, val)

    def op(self, deps, fn, *args, chain=None, **kw):
        for d in deps:
            self.wait(d)
        if (self.chain if chain is None else chain) and self.cnt > 0:
            self.wait((self.sem, self.cnt, self.name))
        ins = fn(*args, **kw)
        self.cnt += 1
        ins.then_inc(self.sem, 1)
        return (self.sem, self.cnt, self.name)

    def op_nomark(self, deps, fn, *args, **kw):
        for d in deps:
            self.wait(d)
        fn(*args, **kw)
        return None

    def last(self):
        if self.sem is None:
            return None
        return (self.sem, self.cnt, self.name)


class Slot:
    def __init__(self, ctx, name):
        name = "%s_%d" % (name, len(ctx.slots))
        self.sem = ctx.new_sem("s_" + name)
        self.cnt = 0
        self.key = "s_" + name
        ctx.slots.append(self)

    def last(self):
        return (self.sem, self.cnt, self.key)


def dma(q, slot, deps, out, in_, **kw):
    for d in deps:
        q.wait(d)
    q.h.dma_start(out=out, in_=in_, **kw).then_inc(slot.sem, 16)
    slot.cnt += 16
    return (slot.sem, slot.cnt, slot.key)


def build(phases, fused):
    nc = bass.Bass("TRN2", target_bir_lowering=False)
    es = ExitStack()
    ctx = Ctx(nc, es)

    class LazyIn:
        def __init__(self, name, shape, dt):
            self.name, self.shape, self.dt, self._ap = name, shape, dt, None

        @property
        def ap(self):
            if self._ap is None:
                self._ap = nc.dram_tensor(self.name, list(self.shape), self.dt, kind="ExternalInput").ap()
                used_inputs.append(self.name)
            return self._ap

        def __getitem__(self, k):
            return self.ap[k]

        def rearrange(self, *a, **k):
            return self.ap.rearrange(*a, **k)

        def partition_broadcast(self, n):
            return self.ap.partition_broadcast(n)

    used_inputs = []

    def dram(name, shape, dt, kind):
        if kind == "ExternalInput":
            return LazyIn(name, shape, dt)
        return nc.dram_tensor(name, list(shape), dt, kind=kind).ap()

    first = phases[0]
    last = phases[-1]
    x_in = dram("x_in", [TOWN, D], F32, "ExternalInput")
    cT = dram("cT", [128, 8], F32, "ExternalInput")
    ident_d = dram("ident", [128, 128], BF16, "ExternalInput")
    ada_w = dram("ada_w", [2, 128, 6, 8, 1024], F32, "ExternalInput")
    ada_b = dram("ada_b", [2, 6144], F32, "ExternalInput")
    norm1_g = dram("norm1_g", [2, D], F32, "ExternalInput")
    norm2_g = dram("norm2_g", [2, D], F32, "ExternalInput")
    wqkv0 = dram("wqkv0", [128, 8, 3072], F32, "ExternalInput")
    wo0 = dram("wo0", [128, 8, 1024], F32, "ExternalInput")
    lamp = dram("lamp", [4, 64], F32, "ExternalInput")
    subln_g = dram("subln_g", [1, 128], F32, "ExternalInput")
    wqkv1 = dram("wqkv1", [128, 8, 1536], F32, "ExternalInput")
    wo1 = dram("wo1", [128, 8, 1024], F32, "ExternalInput")
    qk_g = dram("qk_g", [2, 128], F32, "ExternalInput")
    win = dram("win", [2, 128, NG, 8, 2, 256], F32, "ExternalInput")
    wout = dram("wout", [2, 128, NJ, 1024], F32, "ExternalInput")
    final_g = dram("final_g", [1, D], F32, "ExternalInput")
    ropeA = dram("ropeA", [128, NT, 2, 8], F32, "ExternalInput")
    ropeB = dram("ropeB", [128, NT, 4, 32], F32, "ExternalInput")

    def scratch(name, shape, produced_by, consumed_by):
        if fused:
            return dram(name, shape, BF16, "Internal")
        if produced_by in phases and consumed_by in phases:
            return dram(name, shape, BF16, "Internal")
        if produced_by in phases:
            return dram(name, shape, BF16, "ExternalOutput")
        if consumed_by in phases:
            return dram(name, shape, BF16, "ExternalInput")
        return None

    q0 = scratch("q0", [8, 128, TOWN], "A0", "C0")
    kvo0 = scratch("kvo0", [2048, KVW], "A0", None)
    q1 = scratch("q1", [8, 128, TOWN], "A1", "C1")
    kvo1 = scratch("kvo1", [512, KVW], "A1", None)
    if fused:
        kva0 = dram("kva0", [2 * 2048, KVW], BF16, "Internal")
        kva1 = dram("kva1", [2 * 512, KVW], BF16, "Internal")
        q0b = dram("q0b", [8, 128, TOWN], BF16, "Internal")
        x_oth = dram("x_oth", [TOWN, D], F32, "ExternalInput")
        modsave = dram("modsave", [4, 3 * D], F32, "Internal")
        ropeA_o = dram("ropeA_o", [128, NT, 2, 8], F32, "ExternalInput")
        ropeB_o = dram("ropeB_o", [128, NT, 4, 32], F32, "ExternalInput")
    else:
        kva0 = dram("kva0", [2 * 2048, KVW], BF16, "ExternalInput") if "C0" in phases else None
        kva1 = dram("kva1", [2 * 512, KVW], BF16, "ExternalInput") if "C1" in phases else None
    x_out = None
    if last == "C1" or fused:
        x_out = dram("out", [TOWN, D], F32, "ExternalOutput")
    elif not fused and last == "A1":
        x_out = dram("x_mid", [TOWN, D], F32, "ExternalOutput")

    pe = Engine(ctx, "pe", nc.tensor)
    act = Engine(ctx, "act", nc.scalar)
    dve = Engine(ctx, "dve", nc.vector)
    pool = Engine(ctx, "pool", nc.gpsimd)
    sp = Engine(ctx, "sp", nc.sync, counts=False)
    engines = [pe, act, dve, pool]

    def barrier():
        toks = [e.last() for e in engines] + [s.last() for s in ctx.slots]
        for e in engines + [sp]:
            for t in toks:
                e.wait(t)

    uniq = [0]

    def sb(name, shape, dt, stack=es):
        uniq[0] += 1
        return stack.enter_context(nc.sbuf_tensor("%s_%d" % (name, uniq[0]), list(shape), dt))

    def ps(name, shape, dt, stack):
        uniq[0] += 1
        return stack.enter_context(nc.psum_tensor("%s_%d" % (name, uniq[0]), list(shape), dt))

    x_sb = sb("x_sb", [128, NT, D], F32)
    mod = sb("mod", [128, 2, 3, D], F32)
    ident = sb("ident_sb", [128, 128], BF16)
    ss = sb("ss", [128, NT], F32)
    rstd = sb("rstd", [128, NT], F32)
    lam_t = sb("lam_t", [128, 4], F32)
    junk = sb("junk", [128, D], BF16)
    ada_stack = ExitStack()
    condB = sb("condB", [128, 8, 128], BF16, ada_stack)
    cond = sb("cond", [128, 8], F32, ada_stack)
    ones_f = sb("ones_f", [128, 128], F32, ada_stack)

    s_x = Slot(ctx, "x")
    s_c = Slot(ctx, "c")
    s_w = [Slot(ctx, "w%d" % i) for i in range(4)]
    s_o = [Slot(ctx, "o%d" % i) for i in range(3)]
    s_ov = [Slot(ctx, "ov%d" % i) for i in range(2)]
    s_m = Slot(ctx, "m")
    s_wb = [Slot(ctx, "wb%d" % i) for i in range(2)]
    s_wl = [Slot(ctx, "wl%d" % i) for i in range(2)]
    winb = nc.dram_tensor("winb", [2, 128, NG, 4096], BF16, kind="Internal").ap()
    win_cached = {0: False, 1: False}

    xstate = {}

    def load_x(src):
        xv = src.rearrange("(t p) d -> p t d", p=128)
        for g in range(4):
            dma(sp, s_x, [], x_sb[:, 4 * g:4 * g + 4, :], xv[:, 4 * g:4 * g + 4, :])
        xstate["t"] = s_x.last()
        xstate["rstd_tok"] = None

    load_x(x_in)
    dma(sp, s_c, [], ident[:], ident_d.ap)
    dma(sp, s_c, [], cond[:], cT.ap)
    t_c = s_c.last()
    t_ones = dve.op([], nc.vector.memset, ones_f[:], 1.0)
    t_cond = act.op([t_c], nc.scalar.activation, out=cond[:], in_=cond[:], func=AF.Silu)
    t_cb = None
    for kc in range(8):
        t_cb = dve.op([t_cond, t_ones], nc.vector.tensor_scalar, out=condB[:, kc, :], in0=ones_f[:],
                      scalar1=cond[:, kc:kc + 1], scalar2=None, op0=ALU.mult)

    def adaln(layer, half, gain_dram):
        barrier()
        with ExitStack() as st:
            wb = [sb("adaw%d" % i, [128, 8, 1024], BF16, st) for i in range(2)]
            gB = sb("gB", [128, D], F32, st)
            pacc = ps("ps_ada", [128, 2, 512], F32, st)
            mslot = mod[:, half]
            for j in range(3):
                dma(sp, s_m, [], mslot[:, j, :],
                    ada_b[layer, (half * 3 + j) * D:(half * 3 + j + 1) * D].partition_broadcast(128))
            dma(sp, s_m, [], gB[:], gain_dram.partition_broadcast(128))
            t_m = s_m.last()
            t_add = None
            for j in range(3):
                blk = half * 3 + j
                t_w = dma(pool, s_w[j % 2], [t_add] if j >= 2 else [], wb[j % 2][:], ada_w[layer, :, blk],
                          max_dma_last_dim=8192)
                for cb in range(2):
                    t_mm = None
                    for kc in range(8):
                        t_mm = pe.op([t_w, t_cb, t_add], nc.tensor.matmul, pacc[:, cb, :], lhsT=condB[:, kc, :],
                                     rhs=wb[j % 2][:, kc, cb * 512:(cb + 1) * 512], start=(kc == 0), stop=(kc == 7))
                t_add = dve.op([t_mm, t_m], nc.vector.tensor_tensor, out=mslot[:, j, :], in0=pacc[:].rearrange("p a b -> p (a b)"),
                               in1=mslot[:, j, :], op=ALU.add)
            dve.op([t_add], nc.vector.scalar_tensor_tensor, out=mslot[:, 1, :], in0=mslot[:, 1, :], scalar=1.0,
                   in1=gB[:], op0=ALU.add, op1=ALU.mult)
        barrier()

    def tile_sumsq(dep, i):
        return act.op([dep], nc.scalar.activation, out=junk[:], in_=x_sb[:, i, :], func=AF.Square,
                      accum_out=ss[:, i:i + 1])

    def norm_stats(ntiles=NT, t0=0, have_ss=None):
        if have_ss is None and xstate.get("rstd_tok") is not None:
            t = xstate["rstd_tok"]
            xstate["rstd_tok"] = None
            return t
        t = have_ss
        if have_ss is None:
            for i in range(t0, t0 + ntiles):
                t = tile_sumsq(xstate["t"], i)
        sl = slice(t0, t0 + ntiles)
        t = dve.op([t], nc.vector.tensor_scalar, out=rstd[:, sl], in0=ss[:, sl], scalar1=1.0 / D, scalar2=EPS,
                   op0=ALU.mult, op1=ALU.add)
        t = act.op([t], nc.scalar.activation, out=rstd[:, sl], in_=rstd[:, sl], func=AF.Ln)
        t = act.op([t], nc.scalar.activation, out=rstd[:, sl], in_=rstd[:, sl], func=AF.Exp, scale=-0.5)
        return t

    def stage_A(layer, qd=None, kvo=None, rope=None, need_q=True):
        ncols = 3072 if layer == 0 else 1536
        wq = wqkv0 if layer == 0 else wqkv1
        if qd is None:
            qd = q0 if layer == 0 else q1
        if kvo is None:
            kvo = kvo0 if layer == 0 else kvo1
        if rope is None:
            rope = (ropeA if layer == 0 else ropeB).ap
        nkc = 8 if layer == 0 else 2
        cbs = list(range(ncols // 512))
        if not need_q:
            assert layer == 1
            cbs = [2]
        barrier()
        with ExitStack() as st:
            W = sb("wqkv_sb", [128, 8, ncols], BF16, st)
            tmp = [sb("a_tmp%d" % i, [128, D], F32, st) for i in range(2)]
            hbf = [sb("a_h%d" % i, [128, D], BF16, st) for i in range(2)]
            hT = [sb("a_hT%d" % i, [128, 8, 128], BF16, st) for i in range(2)]
            qkbf = [sb("a_qk%d" % i, [128, 1024 + nkc * 128], BF16, st) for i in range(2)]
            qst = sb("a_qst", [128, 8, 512], BF16, st)
            kst = sb("a_kst", [128, nkc, 512], BF16, st)
            vst2 = [sb("a_vst%d" % i, [128, nkc, 4, VW], BF16, st) for i in range(2)]
            rp = sb("a_rope", [128, NT, 2, 8] if layer == 0 else [128, NT, 4, 32], F32, st)
            rt = [sb("a_rt%d" % i, [128, 16, 8] if layer == 0 else [128, 10, 2, 32], F32, st) for i in range(4)]
            ps_h = ps("ps_h", [128, 8, 128], BF16, st)
            ps_t = ps("ps_t", [128, 8, 128], BF16, st)
            ps_q = ps("ps_qkv", [128, 6, 512], F32, st)
            if layer == 1:
                gq = sb("a_gq", [128, 2, 128], F32, st)
                sq2 = [sb("a_sq%d" % i, [128, 1280], F32, st) for i in range(2)]
                ssq = sb("a_ssq", [128, 10], F32, st)
                rq = sb("a_rq", [128, 10], F32, st)

            kcs = range(8) if need_q else [k_ for k_ in range(8)]
            for kc in range(8):
                if need_q:
                    dma(pool, s_w[2], [], W[:, kc, :], wq[:, kc, :], max_dma_last_dim=4096)
                else:
                    dma(pool, s_w[2], [], W[:, kc, 1024:1536], wq[:, kc, 1024:1536], max_dma_last_dim=2048)
            t_W = s_w[2].last()
            dma(sp, s_c, [], rp[:], rope)
            if layer == 1:
                dma(sp, s_c, [], gq[:].rearrange("p a b -> p (a b)"),
                    qk_g.ap.rearrange("a b -> (a b)").partition_broadcast(128))
            t_rp = s_c.last()
            dve.op([], nc.vector.memset, vst2[0][:, :, :, 128:130], 1.0)
            t_ones_v = dve.op([], nc.vector.memset, vst2[1][:, :, :, 128:130], 1.0)
            t_rstd = norm_stats()
            mslot = mod[:, 0]
            st_out_v = [None, None]

            S = {"tmp": [None, None], "hbf_rd": [None, None], "psh_rd": None, "hT_rd": [None, None],
                 "grp_rd": {}, "qkT_rd": [None, None], "pst_rd": None, "st_out": [None, None, None],
                 "qk_ready": {}, "tv": {}}

            two_sets = (layer == 1)
            t4s = {}

            def psq(t, cb):
                return ps_q[:, (3 * (t % 2) + cb) if two_sets else cb, :]

            def norm(t):
                b = t % 2
                t1 = dve.op([t_rstd, S["tmp"][b]], nc.vector.scalar_tensor_tensor, out=tmp[b][:], in0=x_sb[:, t, :],
                            scalar=rstd[:, t:t + 1], in1=mslot[:, 1, :], op0=ALU.mult, op1=ALU.mult)
                if layer == 0:
                    t2 = pool.op([t1, S["hbf_rd"][b]], nc.gpsimd.tensor_tensor, out=hbf[b][:], in0=tmp[b][:],
                                 in1=mslot[:, 0, :], op=ALU.add)
                else:
                    t2 = dve.op([t1, S["hbf_rd"][b]], nc.vector.tensor_tensor, out=hbf[b][:], in0=tmp[b][:],
                                in1=mslot[:, 0, :], op=ALU.add)
                S["tmp"][b] = t2

            def htr(t):
                b = t % 2
                t3 = None
                for kc in range(8):
                    t3 = pe.op([S["tmp"][b], S["psh_rd"]], nc.tensor.transpose, ps_h[:, kc, :],
                               hbf[b][:, kc * 128:(kc + 1) * 128], ident[:])
                S["hbf_rd"][b] = t3
                t4 = act.op([t3, S["hT_rd"][b]], nc.scalar.copy, out=hT[b][:], in_=ps_h[:])
                S["psh_rd"] = t4
                t4s[t] = t4

            t_mm = {}

            def mm(t, cb):
                b = t % 2
                if layer == 0:
                    grp = cb // 2
                else:
                    grp = t % 2
                tm = None
                for kc in range(8):
                    tm = pe.op([t4s[t], t_W] + S["grp_rd"].get(grp, []), nc.tensor.matmul, psq(t, cb), lhsT=hT[b][:, kc, :],
                               rhs=W[:, kc, cb * 512:(cb + 1) * 512], start=(kc == 0), stop=(kc == 7))
                t_mm[(t, cb)] = tm
                S["hT_rd"][b] = tm

            def post_qk(t, qi):
                b = t % 2
                tcp = act.op([t_mm[(t, 2 * qi + 1)], S["qkT_rd"][b]], nc.scalar.copy, out=qkbf[b][:, qi * 1024:(qi + 1) * 1024],
                             in_=ps_q[:, 2 * qi:2 * qi + 2, :].rearrange("p a b -> p (a b)"))
                cosb = rp[:, t, 0:1, :].to_broadcast([128, 16, 8])
                sinb = rp[:, t, 1:2, :].to_broadcast([128, 16, 8])
                P = ps_q[:, 2 * qi:2 * qi + 2, :].rearrange("p a (h d) -> p (a h) d", d=64)
                x1 = P[:, :, 0:8]
                x2 = P[:, :, 8:16]
                O = qkbf[b][:, qi * 1024:(qi + 1) * 1024].rearrange("p (h d) -> p h d", d=64)
                a1 = dve.op([t_rp, tcp], nc.vector.tensor_tensor, out=rt[0][:], in0=x1, in1=cosb, op=ALU.mult)
                a2 = dve.op([a1], nc.vector.tensor_tensor, out=rt[1][:], in0=x2, in1=sinb, op=ALU.mult)
                a4 = dve.op([a2], nc.vector.tensor_tensor, out=rt[2][:], in0=x2, in1=cosb, op=ALU.mult)
                a5 = dve.op([a4], nc.vector.tensor_tensor, out=rt[3][:], in0=x1, in1=sinb, op=ALU.mult)
                S["grp_rd"][qi] = [a5]
                a3 = dve.op([a5], nc.vector.tensor_tensor, out=O[:, :, 0:8], in0=rt[0][:], in1=rt[1][:], op=ALU.subtract)
                a6 = dve.op([a3], nc.vector.tensor_tensor, out=O[:, :, 8:16], in0=rt[2][:], in1=rt[3][:], op=ALU.add)
                S["qk_ready"].setdefault(t, []).append(a6)

            def post_v0(t):
                g4_, tt = divmod(t, 4)
                vst = vst2[g4_ % 2]
                tv = dve.op([t_mm[(t, 5)], st_out_v[g4_ % 2] if tt == 0 else None], nc.vector.tensor_copy, out=vst[:, :, tt, 0:128],
                            in_=ps_q[:, 4:6, :].rearrange("p a (h d) -> p (a h) d", d=128))
                S["grp_rd"][2] = [tv]
                S["tv"][t] = tv

            def post1(t):
                b = t % 2
                g4_, tt = divmod(t, 4)
                vst = vst2[g4_ % 2]
                base = 3 * (t % 2)
                PQ = ps_q[:, base:base + 3, :].rearrange("p a b -> p (a b)")
                lo = 0 if need_q else 1024
                h0 = lo // 128
                nh = 10 - h0
                sq = sq2[t % 2]
                b1 = act.op([t_mm[(t, 2)]] + S.get(("sq_rd", t % 2), []), nc.scalar.activation, out=sq[:, lo:1280], in_=PQ[:, lo:1280], func=AF.Square)
                b2 = dve.op([b1], nc.vector.tensor_reduce, out=ssq[:, h0:10], in_=sq[:, lo:1280].rearrange("p (h d) -> p h d", d=128),
                            axis=AX.X, op=ALU.add)
                b3 = dve.op([b2], nc.vector.tensor_scalar, out=rq[:, h0:10], in0=ssq[:, h0:10], scalar1=1.0 / 128, scalar2=EPS,
                            op0=ALU.mult, op1=ALU.add)
                b4 = act.op([b3], nc.scalar.activation, out=rq[:, h0:10], in_=rq[:, h0:10], func=AF.Ln)
                b5 = act.op([b4], nc.scalar.activation, out=rq[:, h0:10], in_=rq[:, h0:10], func=AF.Exp, scale=-0.5)
                b6 = dve.op([b5], nc.vector.tensor_tensor, out=sq[:, lo:1280].rearrange("p (h d) -> p h d", d=128),
                            in0=PQ[:, lo:1280].rearrange("p (h d) -> p h d", d=128),
                            in1=rq[:, h0:10].unsqueeze(2).to_broadcast([128, nh, 128]), op=ALU.mult)
                tv = act.op([t_mm[(t, 2)], b6, st_out_v[g4_ % 2] if tt == 0 else None], nc.scalar.copy, out=vst[:, :, tt, 0:128],
                            in_=PQ[:, 1280:1536].rearrange("p (h d) -> p h d", d=128))
                S["grp_rd"][t % 2] = [b6, tv]
                b7 = b6
                if need_q:
                    b7 = dve.op([b6, t_rp], nc.vector.tensor_tensor, out=sq[:, 0:1024].rearrange("p (h d) -> p h d", d=128),
                                in0=sq[:, 0:1024].rearrange("p (h d) -> p h d", d=128),
                                in1=gq[:, 0:1, :].to_broadcast([128, 8, 128]), op=ALU.mult)
                b8 = dve.op([b7, t_rp], nc.vector.tensor_tensor, out=sq[:, 1024:1280].rearrange("p (h d) -> p h d", d=128),
                            in0=sq[:, 1024:1280].rearrange("p (h d) -> p h d", d=128),
                            in1=gq[:, 1:2, :].to_broadcast([128, 2, 128]), op=ALU.mult)
                X = sq[:].rearrange("p (h a b d) -> p h a b d", a=2, b=2, d=32)
                O = qkbf[b][:].rearrange("p (h a b d) -> p h a b d", a=2, b=2, d=32)
                tab = rp[:, t].rearrange("p (a c) d -> p a c d", c=2)
                cosb = tab[:, :, 0, :].unsqueeze(1).to_broadcast([128, nh, 2, 32])
                sinb = tab[:, :, 1, :].unsqueeze(1).to_broadcast([128, nh, 2, 32])
                x1 = X[:, h0:10, :, 0, :]
                x2 = X[:, h0:10, :, 1, :]
                c1 = dve.op([b8, S["qkT_rd"][b]], nc.vector.tensor_tensor, out=rt[0][:, 0:nh], in0=x1, in1=cosb, op=ALU.mult)
                c2 = dve.op([c1], nc.vector.tensor_tensor, out=rt[1][:, 0:nh], in0=x2, in1=sinb, op=ALU.mult)
                c3 = dve.op([c2], nc.vector.tensor_tensor, out=O[:, h0:10, :, 0, :], in0=rt[0][:, 0:nh], in1=rt[1][:, 0:nh], op=ALU.subtract)
                c4 = pool.op([b8, S["qkT_rd"][b]], nc.gpsimd.tensor_tensor, out=rt[2][:, 0:nh], in0=x2, in1=cosb, op=ALU.mult)
                c5 = pool.op([c4], nc.gpsimd.tensor_tensor, out=rt[3][:, 0:nh], in0=x1, in1=sinb, op=ALU.mult)
                c6 = pool.op([c5], nc.gpsimd.tensor_tensor, out=O[:, h0:10, :, 1, :], in0=rt[2][:, 0:nh], in1=rt[3][:, 0:nh], op=ALU.add)
                S["qk_ready"][t] = [c3, c6]
                S[("sq_rd", t % 2)] = [c3, c6]
                S["tv"][t] = tv

            t6s = {}

            def qtr(t):
                b = t % 2
                tt = t % 4
                t5 = None
                for kc in range(8):
                    t5 = pe.op(S["qk_ready"][t] + [S["pst_rd"]], nc.tensor.transpose, ps_t[:, kc, :],
                               qkbf[b][:, kc * 128:(kc + 1) * 128], ident[:])
                t6 = act.op([t5, S["st_out"][0] if tt == 0 else None], nc.scalar.copy,
                            out=qst[:, :, tt * 128:(tt + 1) * 128], in_=ps_t[:])
                S["pst_rd"] = t6
                t6s[t] = t6

            def ktr(t):
                b = t % 2
                g4, tt = divmod(t, 4)
                t7 = None
                for kc in range(nkc):
                    t7 = pe.op(S["qk_ready"][t] + [S["pst_rd"]], nc.tensor.transpose, ps_t[:, kc, :],
                               qkbf[b][:, 1024 + kc * 128:1024 + (kc + 1) * 128], ident[:])
                S["qkT_rd"][b] = t7
                t8 = act.op([t7, S["st_out"][1] if tt == 0 else None], nc.scalar.copy,
                            out=kst[:, :, tt * 128:(tt + 1) * 128], in_=ps_t[:, 0:nkc, :])
                S["pst_rd"] = t8
                if tt == 3:
                    if need_q:
                        S["st_out"][0] = dma(sp, s_o[0], [t6s[t]], qd[:, :, g4 * 512:(g4 + 1) * 512].rearrange("c p t -> p c t"), qst[:])
                    kdst = kvo[0:nkc * 128, g4 * 512:(g4 + 1) * 512].rearrange("(c p) t -> p c t", p=128)
                    S["st_out"][1] = dma(sp, s_o[1], [t8], kdst, kst[:])
                    vdst = kvo[nkc * 128:2 * nkc * 128, g4 * 4 * VW:(g4 + 1) * 4 * VW].rearrange("(c p) (t e) -> p c t e", p=128, e=VW)
                    st_out_v[g4 % 2] = dma(sp, s_ov[g4 % 2], [S["tv"][t], t_ones_v], vdst, vst2[g4 % 2][:])

            ntl = DBG["nt"]
            norm(0)
            htr(0)
            for t in range(ntl):
                nxt = t + 1 < ntl
                prev = t > 0
                if nxt:
                    norm(t + 1)
                if layer == 0:
                    mm(t, 0)
                    mm(t, 1)
                    post_qk(t, 0)
                    if nxt:
                        htr(t + 1)
                    mm(t, 2)
                    mm(t, 3)
                    post_qk(t, 1)
                    if prev:
                        qtr(t - 1)
                    mm(t, 4)
                    mm(t, 5)
                    post_v0(t)
                    if prev:
                        ktr(t - 1)
                elif need_q:
                    mm(t, 0)
                    if nxt:
                        htr(t + 1)
                    mm(t, 1)
                    if prev:
                        qtr(t - 1)
                    mm(t, 2)
                    post1(t)
                    if prev:
                        ktr(t - 1)
                else:
                    mm(t, 2)
                    if nxt:
                        htr(t + 1)
                    post1(t)
                    if prev:
                        ktr(t - 1)
            if need_q:
                qtr(ntl - 1)
            ktr(ntl - 1)
            barrier()

    def compute_lam(st):
        lp = sb("lamp_sb", [128, 4, 64], F32, st)
        lpr = sb("lam_pr", [128, 2, 64], F32, st)
        gsub = sb("gsub", [128, 128], F32, st)
        t0 = dma(sp, s_c, [], lp[:].rearrange("p a b -> p (a b)"), lamp.ap.rearrange("a b -> (a b)").partition_broadcast(128))
        t1 = dma(sp, s_c, [], gsub[:], subln_g.ap.rearrange("a b -> (a b)").partition_broadcast(128))
        a = dve.op([t1], nc.vector.tensor_tensor, out=lpr[:], in0=lp[:, 0:2, :], in1=lp[:, 2:4, :], op=ALU.mult)
        a = dve.op([a], nc.vector.tensor_reduce, out=lam_t[:, 0:2], in_=lpr[:], axis=AX.X, op=ALU.add)
        a = act.op([a], nc.scalar.activation, out=lam_t[:, 0:2], in_=lam_t[:, 0:2], func=AF.Exp)
        a = dve.op([a], nc.vector.tensor_tensor, out=lam_t[:, 2:3], in0=lam_t[:, 0:1], in1=lam_t[:, 1:2], op=ALU.subtract)
        a = dve.op([a], nc.vector.tensor_scalar, out=lam_t[:, 3:4], in0=lam_t[:, 2:3], scalar1=lambda_init_fn(0), scalar2=-1.0,
                   op0=ALU.add, op1=ALU.mult)
        a = dve.op([a], nc.vector.tensor_scalar, out=gsub[:], in0=gsub[:], scalar1=1.0 - lambda_init_fn(0), scalar2=None,
                   op0=ALU.mult)
        return gsub, a

    def stage_CD(layer, qd=None):
        nunits = 8 if layer == 0 else 4
        kva = kva0 if layer == 0 else kva1
        if qd is None:
            qd = q0 if layer == 0 else q1
        nkc = 8 if layer == 0 else 2
        rows = 2 * nkc * 128
        sc = (64 ** -0.5) if layer == 0 else (128 ** -0.5)
        wo_d = wo0 if layer == 0 else wo1
        barrier()
        with ExitStack() as st0:
            o_tm = sb("c_o", [128, NT, D], BF16, st0)
            wo_sb = sb("wo_sb", [128, 8, D], BF16, st0)
            for kc in range(0, 8, 2):
                dma(pool, s_w[3], [], wo_sb[:, kc:kc + 2, :], wo_d[:, kc:kc + 2, :], max_dma_last_dim=4096)
            t_wo = s_w[3].last()
            if DBG.get("units", 99) < 8:
                dve.op([], nc.vector.memset, o_tm[:], 0.0)
            with ExitStack() as st:
                gsub, t_lam = (None, None)
                if layer == 0:
                    gsub, t_lam = compute_lam(st)
                Qb = [sb("c_q%d" % i, [128, 2, TOWN], BF16, st) for i in range(2)]
                t_qz = None
                if layer == 0:
                    for i in range(2):
                        dve.op([], nc.vector.memset, Qb[i][64:128, 0, :], 0.0)
                        t_qz = dve.op([], nc.vector.memset, Qb[i][0:64, 1, :], 0.0)
                Kb = [sb("c_k%d" % i, [128, SEQ], BF16, st) for i in range(2)]
                Vb = [sb("c_v%d" % i, [128, 32, VW], BF16, st) for i in range(2)]
                pt = [sb("c_pt%d" % i, [128, 2, 512], BF16, st) for i in range(3)]
                osb = [sb("c_osb%d" % i, [128, 8, VW], F32, st) for i in range(2)]
                if layer == 0:
                    ubuf = [sb("c_u%d" % i, [128, 4, 128], F32, st) for i in range(2)]
                    fjunk = sb("c_fj", [128, 128], F32, st)
                rec = [sb("c_rec%d" % i, [128, 8], F32, st) for i in range(2)]
                ssq4 = [sb("c_ssq%d" % i, [128, 4], F32, st) for i in range(2)]
                ps_s = ps("ps_s", [128, 2, 2, 512], F32, st)
                ps_o = [ps("ps_o%d" % i, [128, 512], F32, st) for i in range(3)]
                s_ld = [Slot(ctx, "ld%d_%d" % (layer, i)) for i in range(2)]

                units = list(range(nunits))[:DBG.get("units", 99)]
                qbs = list(range(4))[:DBG.get("qbs", 99)]
                kts = list(range(32))[:DBG.get("kts", 99)]
                steps = [(u, qb, kt) for u in units for qb in qbs for kt in kts]
                t_unit_done = {}
                t_loaded = {}

                def load_unit(u):
                    b = u % 2
                    deps = [t_unit_done.get(u - 2)]
                    sl = s_ld[b]
                    if layer == 0:
                        dma(sp, sl, deps, Qb[b][0:64, 0, :], qd[u, 0:64, :])
                        dma(sp, sl, deps, Qb[b][64:128, 1, :], qd[u, 64:128, :])
                        kc = u
                    else:
                        dma(sp, sl, deps, Qb[b][:], qd[2 * u:2 * u + 2].rearrange("c p t -> p c t"))
                        kc = u // 2
                    for r in range(2):
                        dma(sp, sl, deps, Kb[b][:, r * TOWN:(r + 1) * TOWN],
                            kva[r * rows + kc * 128:r * rows + (kc + 1) * 128, 0:TOWN])
                        dma(sp, sl, deps, Vb[b][:, r * 16:(r + 1) * 16, :],
                            kva[r * rows + (nkc + kc) * 128:r * rows + (nkc + kc + 1) * 128, :].rearrange("p (t e) -> p t e", e=VW))
                    t_loaded[u] = sl.last()

                acc_loc = []
                for a in range(8):
                    acc_loc.append((a // 3, (a % 3) * VW))

                t_S = {}
                t_exp = {}
                t_pv = {}
                t_evac = {}
                t_fin_rd = {}
                deferred = []

                def emit_S(i):
                    u, qb, kt = steps[i]
                    b = u % 2
                    sbuf_i = i % 2
                    deps = [t_loaded[u], t_exp.get(i - 2), t_qz]
                    tk = None
                    for lane in range(2):
                        lhsT = Kb[b][:, kt * 128:(kt + 1) * 128]
                        rhs = Qb[b][:, lane, qb * 512:(qb + 1) * 512]
                        tk = (pe.op if lane == 1 else pe.op_nomark)(deps, nc.tensor.matmul, ps_s[:, sbuf_i, lane, :], lhsT=lhsT, rhs=rhs,
                                                                    start=True, stop=True)
                    t_S[i] = tk

                def emit_exp(i):
                    deps = [t_S[i], t_pv.get(i - 3)]
                    t_exp[i] = act.op(deps, nc.scalar.activation, out=pt[i % 3][:].rearrange("p a b -> p (a b)"),
                                      in_=ps_s[:, i % 2].rearrange("p a b -> p (a b)"), func=AF.Exp, scale=sc, chain=False)

                def emit_PV(i, blk):
                    u, qb, kt = steps[i]
                    b = u % 2
                    deps = [t_exp[i]]
                    if kt == kts[0]:
                        deps.append(t_evac.get(blk - 1))
                    tk = None
                    for lane in range(2):
                        for qs in range(4):
                            a = lane * 4 + qs
                            bank, off = acc_loc[a]
                            first_in_bank = (a % 3 == 0)
                            tk = (pe.op if a == 7 else pe.op_nomark)(deps, nc.tensor.matmul, ps_o[bank][:, off:off + VW],
                                       lhsT=pt[i % 3][:, lane, qs * 128:(qs + 1) * 128], rhs=Vb[b][:, kt, :],
                                       start=(kt == kts[0] and first_in_bank), stop=(kt == kts[-1]), skip_group_check=True)
                    t_pv[i] = tk

                def emit_evac(i, blk):
                    bb = blk % 2
                    deps = [t_pv[i], t_fin_rd.get(blk - 2)]
                    t = dve.op(deps, nc.vector.tensor_copy, out=osb[bb][:, 0:3, :].rearrange("p a b -> p (a b)"), in_=ps_o[0][:, 0:3 * VW])
                    t = dve.op([t], nc.vector.tensor_copy, out=osb[bb][:, 3:6, :].rearrange("p a b -> p (a b)"), in_=ps_o[1][:, 0:3 * VW])
                    t = dve.op([t], nc.vector.tensor_copy, out=osb[bb][:, 6:8, :].rearrange("p a b -> p (a b)"), in_=ps_o[2][:, 0:2 * VW])
                    t_evac[blk] = t

                def finalize(i, blk):
                    u, qb, kt = steps[i]
                    bb = blk % 2
                    O = osb[bb]
                    t = dve.op([t_evac[blk]], nc.vector.reciprocal, out=rec[bb][:], in_=O[:, :, 128])
                    if layer == 1:
                        for lane in range(2):
                            for qs in range(4):
                                a = lane * 4 + qs
                                head = 2 * u + lane
                                t = dve.op([t], nc.vector.tensor_scalar, out=o_tm[:, qb * 4 + qs, head * 128:(head + 1) * 128],
                                           in0=O[:, a, 0:128], scalar1=rec[bb][:, a:a + 1], scalar2=None, op0=ALU.mult)
                        t_fin_rd[blk] = t
                        return
                    t = dve.op([t, t_lam], nc.vector.tensor_scalar, out=rec[bb][:, 4:8], in0=rec[bb][:, 4:8], scalar1=lam_t[:, 3:4],
                               scalar2=None, op0=ALU.mult)
                    for qs in range(4):
                        t = dve.op([t], nc.vector.tensor_scalar, out=ubuf[bb][:, qs, :], in0=O[:, qs, 0:128],
                                   scalar1=rec[bb][:, qs:qs + 1], scalar2=None, op0=ALU.mult)
                        t = dve.op([t], nc.vector.scalar_tensor_tensor, out=ubuf[bb][:, qs, :], in0=O[:, 4 + qs, 0:128],
                                   scalar=rec[bb][:, 4 + qs:5 + qs], in1=ubuf[bb][:, qs, :], op0=ALU.mult, op1=ALU.add)
                        t = dve.op([t], nc.vector.scalar_tensor_tensor, out=fjunk[:], in0=ubuf[bb][:, qs, :], scalar=1.0,
                                   in1=ubuf[bb][:, qs, :], op0=ALU.mult, op1=ALU.mult, accum_out=ssq4[bb][:, qs:qs + 1])
                    t = dve.op([t], nc.vector.tensor_scalar, out=ssq4[bb][:], in0=ssq4[bb][:], scalar1=1.0 / 128, scalar2=EPS,
                               op0=ALU.mult, op1=ALU.add)
                    t_fin_rd[blk] = t
                    state = {"t": t}

                    def part_act():
                        a = act.op([state["t"]], nc.scalar.activation, out=ssq4[bb][:], in_=ssq4[bb][:], func=AF.Ln)
                        state["t"] = act.op([a], nc.scalar.activation, out=ssq4[bb][:], in_=ssq4[bb][:], func=AF.Exp, scale=-0.5)

                    def part_dve():
                        t2 = state["t"]
                        for qs in range(4):
                            t2 = dve.op([t2], nc.vector.scalar_tensor_tensor, out=o_tm[:, qb * 4 + qs, u * 128:(u + 1) * 128],
                                        in0=ubuf[bb][:, qs, :], scalar=ssq4[bb][:, qs:qs + 1], in1=gsub[:], op0=ALU.mult, op1=ALU.mult)
                        t_fin_rd[blk] = t2

                    deferred.append((i + (4 if len(kts) >= 16 else 1), part_act))
                    deferred.append((i + (7 if len(kts) >= 16 else 1), part_dve))

                load_unit(units[0])
                if len(units) > 1:
                    load_unit(units[1])
                nsteps = len(steps)
                emit_S(0)
                if nsteps > 1:
                    emit_S(1)
                blk = 0
                for i in range(nsteps):
                    u, qb, kt = steps[i]
                    emit_exp(i)
                    if i + 2 < nsteps:
                        emit_S(i + 2)
                    emit_PV(i, blk)
                    while deferred and deferred[0][0] <= i:
                        deferred.pop(0)[1]()
                    if kt == kts[-1]:
                        emit_evac(i, blk)
                        finalize(i, blk)
                        blk += 1
                        if qb == qbs[-1]:
                            t_unit_done[u] = t_pv[i]
                            ui = units.index(u)
                            if ui + 2 < len(units):
                                load_unit(units[ui + 2])
                while deferred:
                    deferred.pop(0)[1]()
                barrier()
            with ExitStack() as st:
                oT = [sb("d_oT%d" % i, [128, 8, 128], BF16, st) for i in range(2)]
                tmpd = [sb("d_tmp%d" % i, [128, D], F32, st) for i in range(2)]
                ps_tr = [ps("ps_dtr%d" % i, [128, 8, 128], BF16, st) for i in range(2)]
                ps_y = [ps("ps_dy%d" % i, [128, 2, 512], F32, st) for i in range(2)]
                gate = mod[:, 0, 2, :]
                t_cp = [None, None]
                t_mmd = [None, None]
                t_ep = [None, None]
                pend_ssq = []
                for t in range(DBG["nt"]):
                    b = t % 2
                    tt = None
                    for kc in range(8):
                        tt = pe.op([t_cp[b]], nc.tensor.transpose, ps_tr[b][:, kc, :], o_tm[:, t, kc * 128:(kc + 1) * 128], ident[:])
                    t_cp[b] = act.op([tt, t_mmd[b]], nc.scalar.copy, out=oT[b][:], in_=ps_tr[b][:])
                    tm = None
                    for cb in range(2):
                        for kc in range(8):
                            tm = pe.op([t_cp[b], t_wo, t_ep[b]], nc.tensor.matmul, ps_y[b][:, cb, :], lhsT=oT[b][:, kc, :],
                                       rhs=wo_sb[:, kc, cb * 512:(cb + 1) * 512], start=(kc == 0), stop=(kc == 7))
                    t_mmd[b] = tm
                    e1 = dve.op([tm], nc.vector.tensor_tensor, out=tmpd[b][:], in0=ps_y[b][:].rearrange("p a b -> p (a b)"),
                                in1=gate, op=ALU.mult)
                    t_ep[b] = e1
                    e2 = dve.op([e1], nc.vector.tensor_tensor, out=x_sb[:, t, :], in0=x_sb[:, t, :], in1=tmpd[b][:], op=ALU.add)
                    pend_ssq.append((e2, t))
                    if len(pend_ssq) > 1:
                        t_ssq_last = tile_sumsq(*pend_ssq.pop(0))
                while pend_ssq:
                    t_ssq_last = tile_sumsq(*pend_ssq.pop(0))
                if DBG["nt"] == NT:
                    xstate["rstd_tok"] = norm_stats(have_ss=t_ssq_last)
                barrier()

    def stage_E(layer):
        barrier()
        with ExitStack() as st:
            wo_sb = sb("e_wo", [128, NJ, D], BF16, st)
            hTb = sb("e_hT", [128, 8, 512], BF16, st)
            aT = sb("e_aT", [128, NJ, 512], BF16, st)
            wi = [sb("e_wi%d" % i, [128, 8, 2, 256], BF16, st) for i in range(2)]
            sg = [sb("e_sg%d" % i, [128, 512], F32, st) for i in range(2)]
            tmp = [sb("e_tmp%d" % i, [128, D], F32, st) for i in range(2)]
            hbf = [sb("e_h%d" % i, [128, D], BF16, st) for i in range(2)]
            ps_tr = ps("ps_etr", [128, 8, 128], BF16, st)
            ps_gu = ps("ps_gu", [128, 2, 2, 512], F32, st)
            ps_y = ps("ps_ey", [128, 3, 512], F32, st)
            for j in range(0, NJ, 2):
                dma(pool, s_w[3], [], wo_sb[:, j:j + 2, :], wout[layer, :, j:j + 2, :], max_dma_last_dim=4096)
            t_wo = s_w[3].last()
            t_rstd = norm_stats()
            mslot = mod[:, 1]
            gate = mslot[:, 2, :]
            t_wi_rd = [None, None]
            t_wback = [None, None]
            t_gu_rd = [None, None]
            t_y_rd = [None, None, None]
            t_hT_rd = None
            t_aT_rd = None
            t_tmp = [None, None]
            t_hbf_rd = [None, None]
            t_tr_rd = None
            gi = 0
            ji = 0
            yi = 0
            nblk = DBG["nt"] // 4
            for tb in range(nblk):
                t_hT = None
                for tt in range(4):
                    t = tb * 4 + tt
                    b = t % 2
                    t1 = dve.op([t_rstd, t_tmp[b]], nc.vector.scalar_tensor_tensor, out=tmp[b][:], in0=x_sb[:, t, :],
                                scalar=rstd[:, t:t + 1], in1=mslot[:, 1, :], op0=ALU.mult, op1=ALU.mult)
                    t2 = dve.op([t1, t_hbf_rd[b]], nc.vector.tensor_tensor, out=hbf[b][:], in0=tmp[b][:], in1=mslot[:, 0, :], op=ALU.add)
                    t_tmp[b] = t2
                    t3 = None
                    for kc in range(8):
                        t3 = pe.op([t2, t_tr_rd], nc.tensor.transpose, ps_tr[:, kc, :], hbf[b][:, kc * 128:(kc + 1) * 128], ident[:])
                    t_hbf_rd[b] = t3
                    t_hT = act.op([t3, t_hT_rd], nc.scalar.copy, out=hTb[:, :, tt * 128:(tt + 1) * 128], in_=ps_tr[:])
                    t_tr_rd = t_hT
                t_a_last = None
                for g in range(NG):
                    wb = gi % 2
                    wflat = wi[wb][:].rearrange("p a b c -> p (a b c)")
                    if tb == 0 and not win_cached[layer]:
                        t_w = dma(pool, s_w[wb], [t_wi_rd[wb], t_wback[wb]], wflat,
                                  win[layer, :, g].rearrange("p a b c -> p (a b c)"), max_dma_last_dim=4096)
                        t_wback[wb] = dma(sp, s_wb[wb], [t_w], winb[layer, :, g, :], wflat)
                    else:
                        deps = [t_wi_rd[wb], t_wback[wb]]
                        if tb == 1 and g == 0 and not win_cached[layer]:
                            deps += [s_wb[0].last(), s_wb[1].last()]
                        t_w = dma(sp, s_wl[wb], deps, wflat, winb[layer, :, g, :])
                    gi += 1
                    for jj in range(2):
                        j = 2 * g + jj
                        slot = ji % 2
                        ji += 1
                        tmm = [None, None]
                        for gu in range(2):
                            for kc in range(8):
                                tmm[gu] = pe.op([t_w, t_hT, t_gu_rd[slot], t_aT_rd], nc.tensor.matmul, ps_gu[:, slot, gu, :],
                                                lhsT=wi[wb][:, kc, gu, jj * 128:(jj + 1) * 128], rhs=hTb[:, kc, :],
                                                start=(kc == 0), stop=(kc == 7))
                        t_wi_rd[wb] = tmm[1]
                        ts = act.op([tmm[0]], nc.scalar.activation, out=sg[slot][:], in_=ps_gu[:, slot, 0, :], func=AF.Silu)
                        ta = dve.op([ts, tmm[1]], nc.vector.tensor_tensor, out=aT[:, j, :], in0=ps_gu[:, slot, 1, :], in1=sg[slot][:], op=ALU.mult)
                        t_gu_rd[slot] = ta
                        t_a_last = ta
                t_hT_rd = t_wi_rd[(gi - 1) % 2]
                t_last_out = None
                for tt in range(4):
                    t = tb * 4 + tt
                    for cb in range(2):
                        ys = yi % 3
                        yi += 1
                        tm = None
                        for j in range(NJ):
                            tm = pe.op([t_a_last, t_wo, t_y_rd[ys]], nc.tensor.matmul, ps_y[:, ys, :], lhsT=aT[:, j, tt * 128:(tt + 1) * 128],
                                       rhs=wo_sb[:, j, cb * 512:(cb + 1) * 512], start=(j == 0), stop=(j == NJ - 1))
                        t_last_out = tm
                        b = yi % 2
                        e1 = dve.op([tm], nc.vector.tensor_tensor, out=tmp[b][:, 0:512], in0=ps_y[:, ys, :],
                                    in1=gate[:, cb * 512:(cb + 1) * 512], op=ALU.mult)
                        t_y_rd[ys] = e1
                        t_tmp[b] = dve.op([e1], nc.vector.tensor_tensor, out=x_sb[:, t, cb * 512:(cb + 1) * 512],
                                          in0=x_sb[:, t, cb * 512:(cb + 1) * 512], in1=tmp[b][:, 0:512], op=ALU.add)
                        if cb == 1:
                            t_ssq_last = tile_sumsq(t_tmp[b], t)
                t_aT_rd = t_last_out
            win_cached[layer] = True
            if DBG["nt"] == NT:
                xstate["rstd_tok"] = norm_stats(have_ss=t_ssq_last)
            barrier()

    def final_norm():
        barrier()
        with ExitStack() as st:
            gB = sb("f_g", [128, D], F32, st)
            ob = [sb("f_o%d" % i, [128, D], F32, st) for i in range(2)]
            s_f = [Slot(ctx, "f%d" % i) for i in range(2)]
            t_g = dma(sp, s_c, [], gB[:], final_g.ap.rearrange("a b -> (a b)").partition_broadcast(128))
            t_rstd = norm_stats()
            outv = x_out.rearrange("(t p) d -> p t d", p=128)
            t_st = [None, None]
            for t in range(NT):
                b = t % 2
                t1 = dve.op([t_rstd, t_g, t_st[b]], nc.vector.scalar_tensor_tensor, out=ob[b][:], in0=x_sb[:, t, :],
                            scalar=rstd[:, t:t + 1], in1=gB[:], op0=ALU.mult, op1=ALU.mult)
                t_st[b] = dma(sp, s_f[b], [t1], outv[:, t, :], ob[b][:])
            barrier()

    def store_x():
        barrier()
        s_f = Slot(ctx, "xs")
        outv = x_out.rearrange("(t p) d -> p t d", p=128)
        for g in range(4):
            dma(sp, s_f, [], outv[:, 4 * g:4 * g + 4, :], x_sb[:, 4 * g:4 * g + 4, :])
        barrier()

    def adaln_all():
        for idx, (layer, half, gain) in enumerate(((0, 0, norm1_g[0]), (0, 1, norm2_g[0]), (1, 0, norm1_g[1]), (1, 1, norm2_g[1]))):
            adaln(layer, half, gain)
            dma(sp, s_m, [], modsave[idx:idx + 1, :], mod[0:1, half].rearrange("p a b -> p (a b)"))
        barrier()

    def load_mod(idx, half):
        barrier()
        dma(sp, s_m, [], mod[:, half].rearrange("p a b -> p (a b)"), modsave[idx].partition_broadcast(128))
        barrier()

    def exchange(layer):
        barrier()
        kvo = kvo0 if layer == 0 else kvo1
        kva = kva0 if layer == 0 else kva1
        s_cc = Slot(ctx, "cc%d" % layer)
        for d_ in []:
            pass
        pool.h.collective_compute("AllGather", ALU.bypass, replica_groups=[[0, 1], [2, 3], [4, 5], [6, 7]],
                                  ins=[kvo], outs=[kva]).then_inc(s_cc.sem, 16)
        s_cc.cnt += 16
        barrier()

    if fused:
        adaln_all()
        ada_stack.close()
        load_mod(0, 0)
        stage_A(0, qd=q0, kvo=kva0[2048:4096], rope=ropeA.ap)
        barrier()
        load_x(x_oth)
        stage_A(0, qd=q0b, kvo=kva0[0:2048], rope=ropeA_o.ap)
        stage_CD(0, qd=q0b)
        load_mod(1, 1)
        stage_E(0)
        load_mod(2, 0)
        stage_A(1, qd=q1, kvo=kva1[0:512], rope=ropeB_o.ap, need_q=False)
        barrier()
        load_x(x_in)
        load_mod(0, 0)
        stage_CD(0, qd=q0)
        stage_E(0)
        load_mod(2, 0)
        stage_A(1, qd=q1, kvo=kva1[512:1024], rope=ropeB.ap)
        stage_CD(1, qd=q1)
        load_mod(3, 1)
        stage_E(1)
        final_norm()
        phases = []
    for ph in phases:
        if ph == "A0":
            adaln(0, 0, norm1_g[0])
            stage_A(0)
        elif ph == "A1":
            adaln(1, 0, norm1_g[1])
            stage_A(1)
            if last == "A1":
                store_x()
        elif ph == "C0":
            if "A0" not in phases:
                adaln(0, 0, norm1_g[0])
            if "CD" not in DBG.get("skip", ()):
                stage_CD(0)
            adaln(0, 1, norm2_g[0])
            if "E" not in DBG.get("skip", ()):
                stage_E(0)
        elif ph == "C1":
            if "A1" not in phases:
                adaln(1, 0, norm1_g[1])
            if "CD" not in DBG.get("skip", ()):
                stage_CD(1)
            adaln(1, 1, norm2_g[1])
            if "E" not in DBG.get("skip", ()):
                stage_E(1)
            final_norm()
        elif ph == "DBG_ADA":
            adaln(0, 0, norm1_g[0])
            dbg = nc.dram_tensor("dbg", [128, 3 * D], F32, kind="ExternalOutput").ap()
            dma(sp, s_m, [], dbg, mod[:, 0].rearrange("p a b -> p (a b)"))
        elif ph == "DBG_INIT":
            barrier()
            dbg = nc.dram_tensor("dbg", [128, 8 * 128], BF16, kind="ExternalOutput").ap()
            dma(sp, s_m, [], dbg, condB[:].rearrange("p a b -> p (a b)"))
        else:
            raise NotImplementedError(ph)

    barrier()
    if not fused:
        ada_stack.close()
    es.close()
    return nc, used_inputs


def _bf16(a):
    return np.ascontiguousarray(a).astype(ml_dtypes.bfloat16)


def rope_tables():
    t = np.arange(SEQ, dtype=np.int32)
    rows = SEQ // 64
    row_pos = np.broadcast_to(np.arange(rows, dtype=np.int32)[:, None], (rows, 64)).reshape(SEQ)
    col_pos = np.broadcast_to(np.arange(64, dtype=np.int32)[None, :], (rows, 64)).reshape(SEQ)

    def ang(pos, dim, theta):
        half = dim // 2
        freqs = (np.float32(theta) ** (-np.arange(half, dtype=np.float32) / np.float32(half))).astype(np.float32)
        a = pos.astype(np.float32)[:, None] * freqs[None, :]
        return np.cos(a).astype(np.float32), np.sin(a).astype(np.float32)

    ca, sa = ang(t, 16, 500000.0)
    cr, sr = ang(row_pos, 64, 10000.0)
    cc, sc = ang(col_pos, 64, 10000.0)
    A = np.stack([ca, sa], axis=1)
    B = np.stack([cr, sr, cc, sc], axis=1)
    return A, B


def common_inputs(inp):
    f = lambda a: np.ascontiguousarray(np.asarray(a, dtype=np.float32))
    m = {}
    m["ident"] = np.eye(128, dtype=np.float32).astype(ml_dtypes.bfloat16)
    aw = f(inp["ada_w"]).reshape(2, 8, 128, 6, 1024).transpose(0, 2, 3, 1, 4)
    m["ada_w"] = np.ascontiguousarray(aw)
    m["ada_b"] = f(inp["ada_b"])
    m["norm1_g"] = f(inp["norm1_g"])
    m["norm2_g"] = f(inp["norm2_g"])
    m["wqkv0"] = np.ascontiguousarray(f(inp["a_w_qkv"])[0].reshape(8, 128, 3072).transpose(1, 0, 2))
    m["wo0"] = np.ascontiguousarray(f(inp["a_w_o"])[0].reshape(8, 128, 1024).transpose(1, 0, 2))
    m["lamp"] = np.ascontiguousarray(np.concatenate([f(inp["a_lam_q1"]), f(inp["a_lam_q2"]), f(inp["a_lam_k1"]), f(inp["a_lam_k2"])], axis=0))
    m["subln_g"] = f(inp["a_subln_g"])
    m["wqkv1"] = np.ascontiguousarray(f(inp["b_w_qkv"])[0].reshape(8, 128, 1536).transpose(1, 0, 2))
    m["wo1"] = np.ascontiguousarray(f(inp["b_w_o"])[0].reshape(8, 128, 1024).transpose(1, 0, 2))
    m["qk_g"] = np.ascontiguousarray(np.concatenate([f(inp["b_qnorm_g"]), f(inp["b_knorm_g"])], axis=0))
    wi = f(inp["f_w_in"]).reshape(2, 8, 128, 2, NG, 256).transpose(0, 2, 4, 1, 3, 5)
    m["win"] = np.ascontiguousarray(wi)
    wo = f(inp["f_w_out"]).reshape(2, NJ, 128, 1024).transpose(0, 2, 1, 3)
    m["wout"] = np.ascontiguousarray(wo)
    m["final_g"] = f(inp["final_g"]).reshape(1, D)
    return m


def core_inputs(inp, core, A, B):
    b, r = divmod(core, 2)
    x = np.asarray(inp["x"], dtype=np.float32)
    c = np.asarray(inp["c"], dtype=np.float32)
    m = {}
    m["x_in"] = np.ascontiguousarray(x[b, r * TOWN:(r + 1) * TOWN])
    m["cT"] = np.ascontiguousarray(c[b].reshape(8, 128).T)
    sl = slice(r * TOWN, (r + 1) * TOWN)
    m["ropeA"] = np.ascontiguousarray(A[sl].reshape(NT, 128, 2, 8).transpose(1, 0, 2, 3))
    m["ropeB"] = np.ascontiguousarray(B[sl].reshape(NT, 128, 4, 32).transpose(1, 0, 2, 3))
    return m


_PROG_CACHE = {}


def _get_prog(phases, fused):
    key = (tuple(phases), fused)
    if key not in _PROG_CACHE:
        _PROG_CACHE[key] = build(list(phases), fused)
    return _PROG_CACHE[key]


def _run(phases, fused, per_core):
    import time as _t
    t0 = _t.time()
    nc, used = _get_prog(phases, fused)
    in_maps = [{k: m[k] for k in used} for m in per_core]
    t1 = _t.time()
    res = run_bass_kernel_spmd(nc, in_maps, core_ids=list(range(NCORES)))
    print("[kernel] launch %s build %.1fs run %.1fs" % (phases, t1 - t0, _t.time() - t1), flush=True)
    return res.results


def kernel_unfused(**inp):
    A, B = rope_tables()
    cm = common_inputs(inp)
    per_core = []
    for core in range(NCORES):
        m = dict(cm)
        m.update(core_inputs(inp, core, A, B))
        per_core.append(m)
    r1 = _run(["A0"], False, per_core)
    for core in range(NCORES):
        b = core // 2
        per_core[core]["q0"] = r1[core]["q0"]
        per_core[core]["kva0"] = np.concatenate([r1[2 * b]["kvo0"], r1[2 * b + 1]["kvo0"]], axis=0)
    r2 = _run(["C0", "A1"], False, per_core)
    for core in range(NCORES):
        b = core // 2
        per_core[core]["q1"] = r2[core]["q1"]
        per_core[core]["kva1"] = np.concatenate([r2[2 * b]["kvo1"], r2[2 * b + 1]["kvo1"]], axis=0)
        per_core[core]["x_in"] = r2[core]["x_mid"]
    r3 = _run(["C1"], False, per_core)
    out = np.empty((NB, SEQ, D), dtype=np.float32)
    for core in range(NCORES):
        b, r = divmod(core, 2)
        out[b, r * TOWN:(r + 1) * TOWN] = np.asarray(r3[core]["out"], dtype=np.float32)
    return out


def kernel_fused(**inp):
    A, B = rope_tables()
    cm = common_inputs(inp)
    x = np.asarray(inp["x"], dtype=np.float32)
    per_core = []
    for core in range(NCORES):
        b, r = divmod(core, 2)
        m = dict(cm)
        m.update(core_inputs(inp, core, A, B))
        o = 1 - r
        sl = slice(o * TOWN, (o + 1) * TOWN)
        m["x_oth"] = np.ascontiguousarray(x[b, sl])
        m["ropeA_o"] = np.ascontiguousarray(A[sl].reshape(NT, 128, 2, 8).transpose(1, 0, 2, 3))
        m["ropeB_o"] = np.ascontiguousarray(B[sl].reshape(NT, 128, 4, 32).transpose(1, 0, 2, 3))
        per_core.append(m)
    r = _run(["A0", "C0", "A1", "C1"], True, per_core)
    out = np.empty((NB, SEQ, D), dtype=np.float32)
    for core in range(NCORES):
        b, rr = divmod(core, 2)
        out[b, rr * TOWN:(rr + 1) * TOWN] = np.asarray(r[core]["out"], dtype=np.float32)
    return out


def kernel(**inp):
    return kernel_fused(**inp)
```

```python
import math
from contextlib import ExitStack

import numpy as np
import ml_dtypes
import concourse.bass as bass
import concourse.mybir as mybir
from concourse.bass_utils import run_bass_kernel_spmd

F32 = mybir.dt.float32
BF16 = mybir.dt.bfloat16
AF = mybir.ActivationFunctionType
ALU = mybir.AluOpType
AX = mybir.AxisListType

D = 1024
SEQ = 4096
NB = 4
NCORES = 8
TOWN = 2048
NT = 16
DFF = 2816
NJ = 22
NG = 11
EPS = 1e-6
VW = 130
KVW = 16 * VW
DBG = {"lvl": 99, "nt": NT}
SAME_SYNC = True


def lambda_init_fn(layer_idx):
    return 0.8 - 0.6 * math.exp(-0.3 * layer_idx)


class Ctx:
    def __init__(self, nc, es):
        self.nc = nc
        self.es = es
        self.nsem = 0
        self.slots = []

    def new_sem(self, name):
        self.nsem += 1
        return self.es.enter_context(self.nc.semaphore(name))


class Engine:
    def __init__(self, ctx, name, h, counts=True):
        self.h = h
        self.name = name
        self.chain = name in ("act", "dve", "pool")
        self.sem = ctx.new_sem("e_" + name) if counts else None
        self.cnt = 0
        self.seen = {}

    def wait(self, tok):
        if tok is None:
            return
        sem, val, key = tok
        if val <= 0:
            return
        if key == self.name and (self.name == "pe" or not SAME_SYNC):
            return
        if self.seen.get(key, 0) >= val:
            return
        self.seen[key] = val
        self.h.wait_ge(sem, val)

    def op(self, deps, fn, *args, chain=None, **kw):
        for d in deps:
            self.wait(d)
        if (self.chain if chain is None else chain) and self.cnt > 0:
            self.wait((self.sem, self.cnt, self.name))
        ins = fn(*args, **kw)
        self.cnt += 1
        ins.then_inc(self.sem, 1)
        return (self.sem, self.cnt, self.name)

    def op_nomark(self, deps, fn, *args, **kw):
        for d in deps:
            self.wait(d)
        fn(*args, **kw)
        return None

    def last(self):
        if self.sem is None:
            return None
        return (self.sem, self.cnt, self.name)


class Slot:
    def __init__(self, ctx, name):
        name = "%s_%d" % (name, len(ctx.slots))
        self.sem = ctx.new_sem("s_" + name)
        self.cnt = 0
        self.key = "s_" + name
        ctx.slots.append(self)

    def last(self):
        return (self.sem, self.cnt, self.key)


def dma(q, slot, deps, out, in_, **kw):
    for d in deps:
        q.wait(d)
    q.h.dma_start(out=out, in_=in_, **kw).then_inc(slot.sem, 16)
    slot.cnt += 16
    return (slot.sem, slot.cnt, slot.key)


def build(phases, fused):
    nc = bass.Bass("TRN2", target_bir_lowering=False)
    es = ExitStack()
    ctx = Ctx(nc, es)

    class LazyIn:
        def __init__(self, name, shape, dt):
            self.name, self.shape, self.dt, self._ap = name, shape, dt, None

        @property
        def ap(self):
            if self._ap is None:
                self._ap = nc.dram_tensor(self.name, list(self.shape), self.dt, kind="ExternalInput").ap()
                used_inputs.append(self.name)
            return self._ap

        def __getitem__(self, k):
            return self.ap[k]

        def rearrange(self, *a, **k):
            return self.ap.rearrange(*a, **k)

        def partition_broadcast(self, n):
            return self.ap.partition_broadcast(n)

    used_inputs = []

    def dram(name, shape, dt, kind):
        if kind == "ExternalInput":
            return LazyIn(name, shape, dt)
        return nc.dram_tensor(name, list(shape), dt, kind=kind).ap()

    first = phases[0]
    last = phases[-1]
    x_in = dram("x_in", [TOWN, D], F32, "ExternalInput")
    cT = dram("cT", [128, 8], F32, "ExternalInput")
    ident_d = dram("ident", [128, 128], BF16, "ExternalInput")
    ada_w = dram("ada_w", [2, 128, 6, 8, 1024], F32, "ExternalInput")
    ada_b = dram("ada_b", [2, 6144], F32, "ExternalInput")
    norm1_g = dram("norm1_g", [2, D], F32, "ExternalInput")
    norm2_g = dram("norm2_g", [2, D], F32, "ExternalInput")
    wqkv0 = dram("wqkv0", [128, 8, 3072], F32, "ExternalInput")
    wo0 = dram("wo0", [128, 8, 1024], F32, "ExternalInput")
    lamp = dram("lamp", [4, 64], F32, "ExternalInput")
    subln_g = dram("subln_g", [1, 128], F32, "ExternalInput")
    wqkv1 = dram("wqkv1", [128, 8, 1536], F32, "ExternalInput")
    wo1 = dram("wo1", [128, 8, 1024], F32, "ExternalInput")
    qk_g = dram("qk_g", [2, 128], F32, "ExternalInput")
    win = dram("win", [2, 128, NG, 8, 2, 256], F32, "ExternalInput")
    wout = dram("wout", [2, 128, NJ, 1024], F32, "ExternalInput")
    final_g = dram("final_g", [1, D], F32, "ExternalInput")
    ropeA = dram("ropeA", [128, NT, 2, 8], F32, "ExternalInput")
    ropeB = dram("ropeB", [128, NT, 4, 32], F32, "ExternalInput")

    def scratch(name, shape, produced_by, consumed_by):
        if fused:
            return dram(name, shape, BF16, "Internal")
        if produced_by in phases and consumed_by in phases:
            return dram(name, shape, BF16, "Internal")
        if produced_by in phases:
            return dram(name, shape, BF16, "ExternalOutput")
        if consumed_by in phases:
            return dram(name, shape, BF16, "ExternalInput")
        return None

    q0 = scratch("q0", [8, 128, TOWN], "A0", "C0")
    kvo0 = scratch("kvo0", [2048, KVW], "A0", None)
    q1 = scratch("q1", [8, 128, TOWN], "A1", "C1")
    kvo1 = scratch("kvo1", [512, KVW], "A1", None)
    if fused:
        kva0 = dram("kva0", [2 * 2048, KVW], BF16, "Internal")
        kva1 = dram("kva1", [2 * 512, KVW], BF16, "Internal")
        q0b = dram("q0b", [8, 128, TOWN], BF16, "Internal")
        x_oth = dram("x_oth", [TOWN, D], F32, "ExternalInput")
        modsave = dram("modsave", [4, 3 * D], F32, "Internal")
        ropeA_o = dram("ropeA_o", [128, NT, 2, 8], F32, "ExternalInput")
        ropeB_o = dram("ropeB_o", [128, NT, 4, 32], F32, "ExternalInput")
    else:
        kva0 = dram("kva0", [2 * 2048, KVW], BF16, "ExternalInput") if "C0" in phases else None
        kva1 = dram("kva1", [2 * 512, KVW], BF16, "ExternalInput") if "C1" in phases else None
    x_out = None
    if last == "C1" or fused:
        x_out = dram("out", [TOWN, D], F32, "ExternalOutput")
    elif not fused and last == "A1":
        x_out = dram("x_mid", [TOWN, D], F32, "ExternalOutput")

    pe = Engine(ctx, "pe", nc.tensor)
    act = Engine(ctx, "act", nc.scalar)
    dve = Engine(ctx, "dve", nc.vector)
    pool = Engine(ctx, "pool", nc.gpsimd)
    sp = Engine(ctx, "sp", nc.sync, counts=False)
    engines = [pe, act, dve, pool]

    def barrier():
        toks = [e.last() for e in engines] + [s.last() for s in ctx.slots]
        for e in engines + [sp]:
            for t in toks:
                e.wait(t)

    uniq = [0]

    def sb(name, shape, dt, stack=es):
        uniq[0] += 1
        return stack.enter_context(nc.sbuf_tensor("%s_%d" % (name, uniq[0]), list(shape), dt))

    def ps(name, shape, dt, stack):
        uniq[0] += 1
        return stack.enter_context(nc.psum_tensor("%s_%d" % (name, uniq[0]), list(shape), dt))

    x_sb = sb("x_sb", [128, NT, D], F32)
    mod = sb("mod", [128, 2, 3, D], F32)
    ident = sb("ident_sb", [128, 128], BF16)
    ss = sb("ss", [128, NT], F32)
    rstd = sb("rstd", [128, NT], F32)
    lam_t = sb("lam_t", [128, 4], F32)
    junk = sb("junk", [128, D], BF16)
    ada_stack = ExitStack()
    condB = sb("condB", [128, 8, 128], BF16, ada_stack)
    cond = sb("cond", [128, 8], F32, ada_stack)
    ones_f = sb("ones_f", [128, 128], F32, ada_stack)

    s_x = Slot(ctx, "x")
    s_c = Slot(ctx, "c")
    s_w = [Slot(ctx, "w%d" % i) for i in range(4)]
    s_o = [Slot(ctx, "o%d" % i) for i in range(3)]
    s_ov = [Slot(ctx, "ov%d" % i) for i in range(2)]
    s_m = Slot(ctx, "m")
    s_wb = [Slot(ctx, "wb%d" % i) for i in range(2)]
    s_wl = [Slot(ctx, "wl%d" % i) for i in range(2)]
    winb = nc.dram_tensor("winb", [2, 128, NG, 4096], BF16, kind="Internal").ap()
    win_cached = {0: False, 1: False}

    xstate = {}

    def load_x(src):
        xv = src.rearrange("(t p) d -> p t d", p=128)
        for g in range(4):
            dma(sp, s_x, [], x_sb[:, 4 * g:4 * g + 4, :], xv[:, 4 * g:4 * g + 4, :])
        xstate["t"] = s_x.last()
        xstate["rstd_tok"] = None

    load_x(x_in)
    dma(sp, s_c, [], ident[:], ident_d.ap)
    dma(sp, s_c, [], cond[:], cT.ap)
    t_c = s_c.last()
    t_ones = dve.op([], nc.vector.memset, ones_f[:], 1.0)
    t_cond = act.op([t_c], nc.scalar.activation, out=cond[:], in_=cond[:], func=AF.Silu)
    t_cb = None
    for kc in range(8):
        t_cb = dve.op([t_cond, t_ones], nc.vector.tensor_scalar, out=condB[:, kc, :], in0=ones_f[:],
                      scalar1=cond[:, kc:kc + 1], scalar2=None, op0=ALU.mult)

    def adaln(layer, half, gain_dram):
        barrier()
        with ExitStack() as st:
            wb = [sb("adaw%d" % i, [128, 8, 1024], BF16, st) for i in range(2)]
            gB = sb("gB", [128, D], F32, st)
            pacc = ps("ps_ada", [128, 2, 512], F32, st)
            mslot = mod[:, half]
            for j in range(3):
                dma(sp, s_m, [], mslot[:, j, :],
                    ada_b[layer, (half * 3 + j) * D:(half * 3 + j + 1) * D].partition_broadcast(128))
            dma(sp, s_m, [], gB[:], gain_dram.partition_broadcast(128))
            t_m = s_m.last()
            t_add = None
            for j in range(3):
                blk = half * 3 + j
                t_w = dma(pool, s_w[j % 2], [t_add] if j >= 2 else [], wb[j % 2][:], ada_w[layer, :, blk],
                          max_dma_last_dim=8192)
                for cb in range(2):
                    t_mm = None
                    for kc in range(8):
                        t_mm = pe.op([t_w, t_cb, t_add], nc.tensor.matmul, pacc[:, cb, :], lhsT=condB[:, kc, :],
                                     rhs=wb[j % 2][:, kc, cb * 512:(cb + 1) * 512], start=(kc == 0), stop=(kc == 7))
                t_add = dve.op([t_mm, t_m], nc.vector.tensor_tensor, out=mslot[:, j, :], in0=pacc[:].rearrange("p a b -> p (a b)"),
                               in1=mslot[:, j, :], op=ALU.add)
            dve.op([t_add], nc.vector.scalar_tensor_tensor, out=mslot[:, 1, :], in0=mslot[:, 1, :], scalar=1.0,
                   in1=gB[:], op0=ALU.add, op1=ALU.mult)
        barrier()

    def tile_sumsq(dep, i):
        return act.op([dep], nc.scalar.activation, out=junk[:], in_=x_sb[:, i, :], func=AF.Square,
                      accum_out=ss[:, i:i + 1])

    def norm_stats(ntiles=NT, t0=0, have_ss=None):
        if have_ss is None and xstate.get("rstd_tok") is not None:
            t = xstate["rstd_tok"]
            xstate["rstd_tok"] = None
            return t
        t = have_ss
        if have_ss is None:
            for i in range(t0, t0 + ntiles):
                t = tile_sumsq(xstate["t"], i)
        sl = slice(t0, t0 + ntiles)
        t = dve.op([t], nc.vector.tensor_scalar, out=rstd[:, sl], in0=ss[:, sl], scalar1=1.0 / D, scalar2=EPS,
                   op0=ALU.mult, op1=ALU.add)
        t = act.op([t], nc.scalar.activation, out=rstd[:, sl], in_=rstd[:, sl], func=AF.Ln)
        t = act.op([t], nc.scalar.activation, out=rstd[:, sl], in_=rstd[:, sl], func=AF.Exp, scale=-0.5)
        return t

    def stage_A(layer, qd=None, kvo=None, rope=None, need_q=True):
        ncols = 3072 if layer == 0 else 1536
        wq = wqkv0 if layer == 0 else wqkv1
        if qd is None:
            qd = q0 if layer == 0 else q1
        if kvo is None:
            kvo = kvo0 if layer == 0 else kvo1
        if rope is None:
            rope = (ropeA if layer == 0 else ropeB).ap
        nkc = 8 if layer == 0 else 2
        cbs = list(range(ncols // 512))
        if not need_q:
            assert layer == 1
            cbs = [2]
        barrier()
        with ExitStack() as st:
            W = sb("wqkv_sb", [128, 8, ncols], BF16, st)
            tmp = [sb("a_tmp%d" % i, [128, D], F32, st) for i in range(2)]
            hbf = [sb("a_h%d" % i, [128, D], BF16, st) for i in range(2)]
            hT = [sb("a_hT%d" % i, [128, 8, 128], BF16, st) for i in range(2)]
            qkbf = [sb("a_qk%d" % i, [128, 1024 + nkc * 128], BF16, st) for i in range(2)]
            qst = sb("a_qst", [128, 8, 512], BF16, st)
            kst = sb("a_kst", [128, nkc, 512], BF16, st)
            vst2 = [sb("a_vst%d" % i, [128, nkc, 4, VW], BF16, st) for i in range(2)]
            rp = sb("a_rope", [128, NT, 2, 8] if layer == 0 else [128, NT, 4, 32], F32, st)
            rt = [sb("a_rt%d" % i, [128, 16, 8] if layer == 0 else [128, 10, 2, 32], F32, st) for i in range(4)]
            ps_h = ps("ps_h", [128, 8, 128], BF16, st)
            ps_t = ps("ps_t", [128, 8, 128], BF16, st)
            ps_q = ps("ps_qkv", [128, 6, 512], F32, st)
            if layer == 1:
                gq = sb("a_gq", [128, 2, 128], F32, st)
                sq2 = [sb("a_sq%d" % i, [128, 1280], F32, st) for i in range(2)]
                ssq = sb("a_ssq", [128, 10], F32, st)
                rq = sb("a_rq", [128, 10], F32, st)

            kcs = range(8) if need_q else [k_ for k_ in range(8)]
            for kc in range(8):
                if need_q:
                    dma(pool, s_w[2], [], W[:, kc, :], wq[:, kc, :], max_dma_last_dim=4096)
                else:
                    dma(pool, s_w[2], [], W[:, kc, 1024:1536], wq[:, kc, 1024:1536], max_dma_last_dim=2048)
            t_W = s_w[2].last()
            dma(sp, s_c, [], rp[:], rope)
            if layer == 1:
                dma(sp, s_c, [], gq[:].rearrange("p a b -> p (a b)"),
                    qk_g.ap.rearrange("a b -> (a b)").partition_broadcast(128))
            t_rp = s_c.last()
            dve.op([], nc.vector.memset, vst2[0][:, :, :, 128:130], 1.0)
            t_ones_v = dve.op([], nc.vector.memset, vst2[1][:, :, :, 128:130], 1.0)
            t_rstd = norm_stats()
            mslot = mod[:, 0]
            st_out_v = [None, None]

            S = {"tmp": [None, None], "hbf_rd": [None, None], "psh_rd": None, "hT_rd": [None, None],
                 "grp_rd": {}, "qkT_rd": [None, None], "pst_rd": None, "st_out": [None, None, None],
                 "qk_ready": {}, "tv": {}}

            two_sets = (layer == 1)
            t4s = {}

            def psq(t, cb):
                return ps_q[:, (3 * (t % 2) + cb) if two_sets else cb, :]

            def norm(t):
                b = t % 2
                t1 = dve.op([t_rstd, S["tmp"][b]], nc.vector.scalar_tensor_tensor, out=tmp[b][:], in0=x_sb[:, t, :],
                            scalar=rstd[:, t:t + 1], in1=mslot[:, 1, :], op0=ALU.mult, op1=ALU.mult)
                if layer == 0:
                    t2 = pool.op([t1, S["hbf_rd"][b]], nc.gpsimd.tensor_tensor, out=hbf[b][:], in0=tmp[b][:],
                                 in1=mslot[:, 0, :], op=ALU.add)
                else:
                    t2 = dve.op([t1, S["hbf_rd"][b]], nc.vector.tensor_tensor, out=hbf[b][:], in0=tmp[b][:],
                                in1=mslot[:, 0, :], op=ALU.add)
                S["tmp"][b] = t2

            def htr(t):
                b = t % 2
                t3 = None
                for kc in range(8):
                    t3 = pe.op([S["tmp"][b], S["psh_rd"]], nc.tensor.transpose, ps_h[:, kc, :],
                               hbf[b][:, kc * 128:(kc + 1) * 128], ident[:])
                S["hbf_rd"][b] = t3
                t4 = act.op([t3, S["hT_rd"][b]], nc.scalar.copy, out=hT[b][:], in_=ps_h[:])
                S["psh_rd"] = t4
                t4s[t] = t4

            t_mm = {}

            def mm(t, cb):
                b = t % 2
                if layer == 0:
                    grp = cb // 2
                else:
                    grp = t % 2
                tm = None
                for kc in range(8):
                    tm = pe.op([t4s[t], t_W] + S["grp_rd"].get(grp, []), nc.tensor.matmul, psq(t, cb), lhsT=hT[b][:, kc, :],
                               rhs=W[:, kc, cb * 512:(cb + 1) * 512], start=(kc == 0), stop=(kc == 7))
                t_mm[(t, cb)] = tm
                S["hT_rd"][b] = tm

            def post_qk(t, qi):
                b = t % 2
                tcp = act.op([t_mm[(t, 2 * qi + 1)], S["qkT_rd"][b]], nc.scalar.copy, out=qkbf[b][:, qi * 1024:(qi + 1) * 1024],
                             in_=ps_q[:, 2 * qi:2 * qi + 2, :].rearrange("p a b -> p (a b)"))
                cosb = rp[:, t, 0:1, :].to_broadcast([128, 16, 8])
                sinb = rp[:, t, 1:2, :].to_broadcast([128, 16, 8])
                P = ps_q[:, 2 * qi:2 * qi + 2, :].rearrange("p a (h d) -> p (a h) d", d=64)
                x1 = P[:, :, 0:8]
                x2 = P[:, :, 8:16]
                O = qkbf[b][:, qi * 1024:(qi + 1) * 1024].rearrange("p (h d) -> p h d", d=64)
                a1 = dve.op([t_rp, tcp], nc.vector.tensor_tensor, out=rt[0][:], in0=x1, in1=cosb, op=ALU.mult)
                a2 = dve.op([a1], nc.vector.tensor_tensor, out=rt[1][:], in0=x2, in1=sinb, op=ALU.mult)
                a4 = dve.op([a2], nc.vector.tensor_tensor, out=rt[2][:], in0=x2, in1=cosb, op=ALU.mult)
                a5 = dve.op([a4], nc.vector.tensor_tensor, out=rt[3][:], in0=x1, in1=sinb, op=ALU.mult)
                S["grp_rd"][qi] = [a5]
                a3 = dve.op([a5], nc.vector.tensor_tensor, out=O[:, :, 0:8], in0=rt[0][:], in1=rt[1][:], op=ALU.subtract)
                a6 = dve.op([a3], nc.vector.tensor_tensor, out=O[:, :, 8:16], in0=rt[2][:], in1=rt[3][:], op=ALU.add)
                S["qk_ready"].setdefault(t, []).append(a6)

            def post_v0(t):
                g4_, tt = divmod(t, 4)
                vst = vst2[g4_ % 2]
                tv = dve.op([t_mm[(t, 5)], st_out_v[g4_ % 2] if tt == 0 else None], nc.vector.tensor_copy, out=vst[:, :, tt, 0:128],
                            in_=ps_q[:, 4:6, :].rearrange("p a (h d) -> p (a h) d", d=128))
                S["grp_rd"][2] = [tv]
                S["tv"][t] = tv

            def post1(t):
                b = t % 2
                g4_, tt = divmod(t, 4)
                vst = vst2[g4_ % 2]
                base = 3 * (t % 2)
                PQ = ps_q[:, base:base + 3, :].rearrange("p a b -> p (a b)")
                lo = 0 if need_q else 1024
                h0 = lo // 128
                nh = 10 - h0
                sq = sq2[t % 2]
                b1 = act.op([t_mm[(t, 2)]] + S.get(("sq_rd", t % 2), []), nc.scalar.activation, out=sq[:, lo:1280], in_=PQ[:, lo:1280], func=AF.Square)
                b2 = dve.op([b1], nc.vector.tensor_reduce, out=ssq[:, h0:10], in_=sq[:, lo:1280].rearrange("p (h d) -> p h d", d=128),
                            axis=AX.X, op=ALU.add)
                b3 = dve.op([b2], nc.vector.tensor_scalar, out=rq[:, h0:10], in0=ssq[:, h0:10], scalar1=1.0 / 128, scalar2=EPS,
                            op0=ALU.mult, op1=ALU.add)
                b4 = act.op([b3], nc.scalar.activation, out=rq[:, h0:10], in_=rq[:, h0:10], func=AF.Ln)
                b5 = act.op([b4], nc.scalar.activation, out=rq[:, h0:10], in_=rq[:, h0:10], func=AF.Exp, scale=-0.5)
                b6 = dve.op([b5], nc.vector.tensor_tensor, out=sq[:, lo:1280].rearrange("p (h d) -> p h d", d=128),
                            in0=PQ[:, lo:1280].rearrange("p (h d) -> p h d", d=128),
                            in1=rq[:, h0:10].unsqueeze(2).to_broadcast([128, nh, 128]), op=ALU.mult)
                tv = act.op([t_mm[(t, 2)], b6, st_out_v[g4_ % 2] if tt == 0 else None], nc.scalar.copy, out=vst[:, :, tt, 0:128],
                            in_=PQ[:, 1280:1536].rearrange("p (h d) -> p h d", d=128))
                S["grp_rd"][t % 2] = [b6, tv]
                b7 = b6
                if need_q:
                    b7 = dve.op([b6, t_rp], nc.vector.tensor_tensor, out=sq[:, 0:1024].rearrange("p (h d) -> p h d", d=128),
                                in0=sq[:, 0:1024].rearrange("p (h d) -> p h d", d=128),
                                in1=gq[:, 0:1, :].to_broadcast([128, 8, 128]), op=ALU.mult)
                b8 = dve.op([b7, t_rp], nc.vector.tensor_tensor, out=sq[:, 1024:1280].rearrange("p (h d) -> p h d", d=128),
                            in0=sq[:, 1024:1280].rearrange("p (h d) -> p h d", d=128),
                            in1=gq[:, 1:2, :].to_broadcast([128, 2, 128]), op=ALU.mult)
                X = sq[:].rearrange("p (h a b d) -> p h a b d", a=2, b=2, d=32)
                O = qkbf[b][:].rearrange("p (h a b d) -> p h a b d", a=2, b=2, d=32)
                tab = rp[:, t].rearrange("p (a c) d -> p a c d", c=2)
                cosb = tab[:, :, 0, :].unsqueeze(1).to_broadcast([128, nh, 2, 32])
                sinb = tab[:, :, 1, :].unsqueeze(1).to_broadcast([128, nh, 2, 32])
                x1 = X[:, h0:10, :, 0, :]
                x2 = X[:, h0:10, :, 1, :]
                c1 = dve.op([b8, S["qkT_rd"][b]], nc.vector.tensor_tensor, out=rt[0][:, 0:nh], in0=x1, in1=cosb, op=ALU.mult)
                c2 = dve.op([c1], nc.vector.tensor_tensor, out=rt[1][:, 0:nh], in0=x2, in1=sinb, op=ALU.mult)
                c3 = dve.op([c2], nc.vector.tensor_tensor, out=O[:, h0:10, :, 0, :], in0=rt[0][:, 0:nh], in1=rt[1][:, 0:nh], op=ALU.subtract)
                c4 = pool.op([b8, S["qkT_rd"][b]], nc.gpsimd.tensor_tensor, out=rt[2][:, 0:nh], in0=x2, in1=cosb, op=ALU.mult)
                c5 = pool.op([c4], nc.gpsimd.tensor_tensor, out=rt[3][:, 0:nh], in0=x1, in1=sinb, op=ALU.mult)
                c6 = pool.op([c5], nc.gpsimd.tensor_tensor, out=O[:, h0:10, :, 1, :], in0=rt[2][:, 0:nh], in1=rt[3][:, 0:nh], op=ALU.add)
                S["qk_ready"][t] = [c3, c6]
                S[("sq_rd", t % 2)] = [c3, c6]
                S["tv"][t] = tv

            t6s = {}

            def qtr(t):
                b = t % 2
                tt = t % 4
                t5 = None
                for kc in range(8):
                    t5 = pe.op(S["qk_ready"][t] + [S["pst_rd"]], nc.tensor.transpose, ps_t[:, kc, :],
                               qkbf[b][:, kc * 128:(kc + 1) * 128], ident[:])
                t6 = act.op([t5, S["st_out"][0] if tt == 0 else None], nc.scalar.copy,
                            out=qst[:, :, tt * 128:(tt + 1) * 128], in_=ps_t[:])
                S["pst_rd"] = t6
                t6s[t] = t6

            def ktr(t):
                b = t % 2
                g4, tt = divmod(t, 4)
                t7 = None
                for kc in range(nkc):
                    t7 = pe.op(S["qk_ready"][t] + [S["pst_rd"]], nc.tensor.transpose, ps_t[:, kc, :],
                               qkbf[b][:, 1024 + kc * 128:1024 + (kc + 1) * 128], ident[:])
                S["qkT_rd"][b] = t7
                t8 = act.op([t7, S["st_out"][1] if tt == 0 else None], nc.scalar.copy,
                            out=kst[:, :, tt * 128:(tt + 1) * 128], in_=ps_t[:, 0:nkc, :])
                S["pst_rd"] = t8
                if tt == 3:
                    if need_q:
                        S["st_out"][0] = dma(sp, s_o[0], [t6s[t]], qd[:, :, g4 * 512:(g4 + 1) * 512].rearrange("c p t -> p c t"), qst[:])
                    kdst = kvo[0:nkc * 128, g4 * 512:(g4 + 1) * 512].rearrange("(c p) t -> p c t", p=128)
                    S["st_out"][1] = dma(sp, s_o[1], [t8], kdst, kst[:])
                    vdst = kvo[nkc * 128:2 * nkc * 128, g4 * 4 * VW:(g4 + 1) * 4 * VW].rearrange("(c p) (t e) -> p c t e", p=128, e=VW)
                    st_out_v[g4 % 2] = dma(sp, s_ov[g4 % 2], [S["tv"][t], t_ones_v], vdst, vst2[g4 % 2][:])

            ntl = DBG["nt"]
            norm(0)
            htr(0)
            for t in range(ntl):
                nxt = t + 1 < ntl
                prev = t > 0
                if nxt:
                    norm(t + 1)
                if layer == 0:
                    mm(t, 0)
                    mm(t, 1)
                    post_qk(t, 0)
                    if nxt:
                        htr(t + 1)
                    mm(t, 2)
                    mm(t, 3)
                    post_qk(t, 1)
                    if prev:
                        qtr(t - 1)
                    mm(t, 4)
                    mm(t, 5)
                    post_v0(t)
                    if prev:
                        ktr(t - 1)
                elif need_q:
                    mm(t, 0)
                    if nxt:
                        htr(t + 1)
                    mm(t, 1)
                    if prev:
                        qtr(t - 1)
                    mm(t, 2)
                    post1(t)
                    if prev:
                        ktr(t - 1)
                else:
                    mm(t, 2)
                    if nxt:
                        htr(t + 1)
                    post1(t)
                    if prev:
                        ktr(t - 1)
            if need_q:
                qtr(ntl - 1)
            ktr(ntl - 1)
            barrier()

    def compute_lam(st):
        lp = sb("lamp_sb", [128, 4, 64], F32, st)
        lpr = sb("lam_pr", [128, 2, 64], F32, st)
        gsub = sb("gsub", [128, 128], F32, st)
        t0 = dma(sp, s_c, [], lp[:].rearrange("p a b -> p (a b)"), lamp.ap.rearrange("a b -> (a b)").partition_broadcast(128))
        t1 = dma(sp, s_c, [], gsub[:], subln_g.ap.rearrange("a b -> (a b)").partition_broadcast(128))
        a = dve.op([t1], nc.vector.tensor_tensor, out=lpr[:], in0=lp[:, 0:2, :], in1=lp[:, 2:4, :], op=ALU.mult)
        a = dve.op([a], nc.vector.tensor_reduce, out=lam_t[:, 0:2], in_=lpr[:], axis=AX.X, op=ALU.add)
        a = act.op([a], nc.scalar.activation, out=lam_t[:, 0:2], in_=lam_t[:, 0:2], func=AF.Exp)
        a = dve.op([a], nc.vector.tensor_tensor, out=lam_t[:, 2:3], in0=lam_t[:, 0:1], in1=lam_t[:, 1:2], op=ALU.subtract)
        a = dve.op([a], nc.vector.tensor_scalar, out=lam_t[:, 3:4], in0=lam_t[:, 2:3], scalar1=lambda_init_fn(0), scalar2=-1.0,
                   op0=ALU.add, op1=ALU.mult)
        a = dve.op([a], nc.vector.tensor_scalar, out=gsub[:], in0=gsub[:], scalar1=1.0 - lambda_init_fn(0), scalar2=None,
                   op0=ALU.mult)
        return gsub, a

    def stage_CD(layer, qd=None):
        nunits = 8 if layer == 0 else 4
        kva = kva0 if layer == 0 else kva1
        if qd is None:
            qd = q0 if layer == 0 else q1
        nkc = 8 if layer == 0 else 2
        rows = 2 * nkc * 128
        sc = (64 ** -0.5) if layer == 0 else (128 ** -0.5)
        wo_d = wo0 if layer == 0 else wo1
        barrier()
        with ExitStack() as st0:
            o_tm = sb("c_o", [128, NT, D], BF16, st0)
            wo_sb = sb("wo_sb", [128, 8, D], BF16, st0)
            for kc in range(0, 8, 2):
                dma(pool, s_w[3], [], wo_sb[:, kc:kc + 2, :], wo_d[:, kc:kc + 2, :], max_dma_last_dim=4096)
            t_wo = s_w[3].last()
            if DBG.get("units", 99) < 8:
                dve.op([], nc.vector.memset, o_tm[:], 0.0)
            with ExitStack() as st:
                gsub, t_lam = (None, None)
                if layer == 0:
                    gsub, t_lam = compute_lam(st)
                Qb = [sb("c_q%d" % i, [128, 2, TOWN], BF16, st) for i in range(2)]
                t_qz = None
                if layer == 0:
                    for i in range(2):
                        dve.op([], nc.vector.memset, Qb[i][64:128, 0, :], 0.0)
                        t_qz = dve.op([], nc.vector.memset, Qb[i][0:64, 1, :], 0.0)
                Kb = [sb("c_k%d" % i, [128, SEQ], BF16, st) for i in range(2)]
                Vb = [sb("c_v%d" % i, [128, 32, VW], BF16, st) for i in range(2)]
                pt = [sb("c_pt%d" % i, [128, 2, 512], BF16, st) for i in range(3)]
                osb = [sb("c_osb%d" % i, [128, 8, VW], F32, st) for i in range(2)]
                if layer == 0:
                    ubuf = [sb("c_u%d" % i, [128, 4, 128], F32, st) for i in range(2)]
                    fjunk = sb("c_fj", [128, 128], F32, st)
                rec = [sb("c_rec%d" % i, [128, 8], F32, st) for i in range(2)]
                ssq4 = [sb("c_ssq%d" % i, [128, 4], F32, st) for i in range(2)]
                ps_s = ps("ps_s", [128, 2, 2, 512], F32, st)
                ps_o = [ps("ps_o%d" % i, [128, 512], F32, st) for i in range(3)]
                s_ld = [Slot(ctx, "ld%d_%d" % (layer, i)) for i in range(2)]

                units = list(range(nunits))[:DBG.get("units", 99)]
                qbs = list(range(4))[:DBG.get("qbs", 99)]
                kts = list(range(32))[:DBG.get("kts", 99)]
                steps = [(u, qb, kt) for u in units for qb in qbs for kt in kts]
                t_unit_done = {}
                t_loaded = {}

                def load_unit(u):
                    b = u % 2
                    deps = [t_unit_done.get(u - 2)]
                    sl = s_ld[b]
                    if layer == 0:
                        dma(sp, sl, deps, Qb[b][0:64, 0, :], qd[u, 0:64, :])
                        dma(sp, sl, deps, Qb[b][64:128, 1, :], qd[u, 64:128, :])
                        kc = u
                    else:
                        dma(sp, sl, deps, Qb[b][:], qd[2 * u:2 * u + 2].rearrange("c p t -> p c t"))
                        kc = u // 2
                    for r in range(2):
                        dma(sp, sl, deps, Kb[b][:, r * TOWN:(r + 1) * TOWN],
                            kva[r * rows + kc * 128:r * rows + (kc + 1) * 128, 0:TOWN])
                        dma(sp, sl, deps, Vb[b][:, r * 16:(r + 1) * 16, :],
                            kva[r * rows + (nkc + kc) * 128:r * rows + (nkc + kc + 1) * 128, :].rearrange("p (t e) -> p t e", e=VW))
                    t_loaded[u] = sl.last()

                acc_loc = []
                for a in range(8):
                    acc_loc.append((a // 3, (a % 3) * VW))

                t_S = {}
                t_exp = {}
                t_pv = {}
                t_evac = {}
                t_fin_rd = {}
                deferred = []

                def emit_S(i):
                    u, qb, kt = steps[i]
                    b = u % 2
                    sbuf_i = i % 2
                    deps = [t_loaded[u], t_exp.get(i - 2), t_qz]
                    tk = None
                    for lane in range(2):
                        lhsT = Kb[b][:, kt * 128:(kt + 1) * 128]
                        rhs = Qb[b][:, lane, qb * 512:(qb + 1) * 512]
                        tk = (pe.op if lane == 1 else pe.op_nomark)(deps, nc.tensor.matmul, ps_s[:, sbuf_i, lane, :], lhsT=lhsT, rhs=rhs,
                                                                    start=True, stop=True)
                    t_S[i] = tk

                def emit_exp(i):
                    deps = [t_S[i], t_pv.get(i - 3)]
                    t_exp[i] = act.op(deps, nc.scalar.activation, out=pt[i % 3][:].rearrange("p a b -> p (a b)"),
                                      in_=ps_s[:, i % 2].rearrange("p a b -> p (a b)"), func=AF.Exp, scale=sc, chain=False)

                def emit_PV(i, blk):
                    u, qb, kt = steps[i]
                    b = u % 2
                    tk = None
                    for lane in range(2):
                        for qs in range(4):
                            a = lane * 4 + qs
                            bank, off = acc_loc[a]
                            first_in_bank = (a % 3 == 0)
                            deps = [t_exp[i]]
                            if kt == kts[0] and (blk - 1) in t_evac:
                                deps.append(t_evac[blk - 1][bank])
                            tk = (pe.op if a == 7 else pe.op_nomark)(deps, nc.tensor.matmul, ps_o[bank][:, off:off + VW],
                                       lhsT=pt[i % 3][:, lane, qs * 128:(qs + 1) * 128], rhs=Vb[b][:, kt, :],
                                       start=(kt == kts[0] and first_in_bank), stop=(kt == kts[-1]), skip_group_check=True)
                    t_pv[i] = tk

                def emit_evac(i, blk):
                    bb = blk % 2
                    deps = [t_pv[i], t_fin_rd.get(blk - 2)]
                    ta = dve.op(deps, nc.vector.tensor_copy, out=osb[bb][:, 0:3, :].rearrange("p a b -> p (a b)"), in_=ps_o[0][:, 0:3 * VW])
                    tb_ = dve.op([ta], nc.vector.tensor_copy, out=osb[bb][:, 3:6, :].rearrange("p a b -> p (a b)"), in_=ps_o[1][:, 0:3 * VW])
                    tc = dve.op([tb_], nc.vector.tensor_copy, out=osb[bb][:, 6:8, :].rearrange("p a b -> p (a b)"), in_=ps_o[2][:, 0:2 * VW])
                    t_evac[blk] = [ta, tb_, tc]

                def finalize(i, blk):
                    u, qb, kt = steps[i]
                    bb = blk % 2
                    O = osb[bb]
                    t = dve.op([t_evac[blk][2]], nc.vector.reciprocal, out=rec[bb][:], in_=O[:, :, 128])
                    if layer == 1:
                        for lane in range(2):
                            for qs in range(4):
                                a = lane * 4 + qs
                                head = 2 * u + lane
                                t = dve.op([t], nc.vector.tensor_scalar, out=o_tm[:, qb * 4 + qs, head * 128:(head + 1) * 128],
                                           in0=O[:, a, 0:128], scalar1=rec[bb][:, a:a + 1], scalar2=None, op0=ALU.mult)
                        t_fin_rd[blk] = t
                        return
                    t = dve.op([t, t_lam], nc.vector.tensor_scalar, out=rec[bb][:, 4:8], in0=rec[bb][:, 4:8], scalar1=lam_t[:, 3:4],
                               scalar2=None, op0=ALU.mult)
                    for qs in range(4):
                        t = dve.op([t], nc.vector.tensor_scalar, out=ubuf[bb][:, qs, :], in0=O[:, qs, 0:128],
                                   scalar1=rec[bb][:, qs:qs + 1], scalar2=None, op0=ALU.mult)
                        t = dve.op([t], nc.vector.scalar_tensor_tensor, out=ubuf[bb][:, qs, :], in0=O[:, 4 + qs, 0:128],
                                   scalar=rec[bb][:, 4 + qs:5 + qs], in1=ubuf[bb][:, qs, :], op0=ALU.mult, op1=ALU.add)
                        t = dve.op([t], nc.vector.scalar_tensor_tensor, out=fjunk[:], in0=ubuf[bb][:, qs, :], scalar=1.0,
                                   in1=ubuf[bb][:, qs, :], op0=ALU.mult, op1=ALU.mult, accum_out=ssq4[bb][:, qs:qs + 1])
                    t = dve.op([t], nc.vector.tensor_scalar, out=ssq4[bb][:], in0=ssq4[bb][:], scalar1=1.0 / 128, scalar2=EPS,
                               op0=ALU.mult, op1=ALU.add)
                    t_fin_rd[blk] = t
                    state = {"t": t}

                    def part_act():
                        a = act.op([state["t"]], nc.scalar.activation, out=ssq4[bb][:], in_=ssq4[bb][:], func=AF.Ln)
                        state["t"] = act.op([a], nc.scalar.activation, out=ssq4[bb][:], in_=ssq4[bb][:], func=AF.Exp, scale=-0.5)

                    def part_dve():
                        t2 = state["t"]
                        for qs in range(4):
                            t2 = dve.op([t2], nc.vector.scalar_tensor_tensor, out=o_tm[:, qb * 4 + qs, u * 128:(u + 1) * 128],
                                        in0=ubuf[bb][:, qs, :], scalar=ssq4[bb][:, qs:qs + 1], in1=gsub[:], op0=ALU.mult, op1=ALU.mult)
                        t_fin_rd[blk] = t2

                    deferred.append((i + (4 if len(kts) >= 16 else 1), part_act))
                    deferred.append((i + (7 if len(kts) >= 16 else 1), part_dve))

                load_unit(units[0])
                if len(units) > 1:
                    load_unit(units[1])
                nsteps = len(steps)
                emit_S(0)
                if nsteps > 1:
                    emit_S(1)
                blk = 0
                for i in range(nsteps):
                    u, qb, kt = steps[i]
                    emit_exp(i)
                    if i + 2 < nsteps:
                        emit_S(i + 2)
                    emit_PV(i, blk)
                    while deferred and deferred[0][0] <= i:
                        deferred.pop(0)[1]()
                    if kt == kts[-1]:
                        emit_evac(i, blk)
                        finalize(i, blk)
                        blk += 1
                        if qb == qbs[-1]:
                            t_unit_done[u] = t_pv[i]
                            ui = units.index(u)
                            if ui + 2 < len(units):
                                load_unit(units[ui + 2])
                while deferred:
                    deferred.pop(0)[1]()
                barrier()
            with ExitStack() as st:
                oT = [sb("d_oT%d" % i, [128, 8, 128], BF16, st) for i in range(2)]
                tmpd = [sb("d_tmp%d" % i, [128, D], F32, st) for i in range(2)]
                ps_tr = [ps("ps_dtr%d" % i, [128, 8, 128], BF16, st) for i in range(2)]
                ps_y = [ps("ps_dy%d" % i, [128, 2, 512], F32, st) for i in range(2)]
                gate = mod[:, 0, 2, :]
                t_cp = [None, None]
                t_mmd = [None, None]
                t_ep = [None, None]
                pend_ssq = []
                for t in range(DBG["nt"]):
                    b = t % 2
                    tt = None
                    for kc in range(8):
                        tt = pe.op([t_cp[b]], nc.tensor.transpose, ps_tr[b][:, kc, :], o_tm[:, t, kc * 128:(kc + 1) * 128], ident[:])
                    t_cp[b] = act.op([tt, t_mmd[b]], nc.scalar.copy, out=oT[b][:], in_=ps_tr[b][:])
                    tm = None
                    for cb in range(2):
                        for kc in range(8):
                            tm = pe.op([t_cp[b], t_wo, t_ep[b]], nc.tensor.matmul, ps_y[b][:, cb, :], lhsT=oT[b][:, kc, :],
                                       rhs=wo_sb[:, kc, cb * 512:(cb + 1) * 512], start=(kc == 0), stop=(kc == 7))
                    t_mmd[b] = tm
                    e1 = dve.op([tm], nc.vector.tensor_tensor, out=tmpd[b][:], in0=ps_y[b][:].rearrange("p a b -> p (a b)"),
                                in1=gate, op=ALU.mult)
                    t_ep[b] = e1
                    e2 = dve.op([e1], nc.vector.tensor_tensor, out=x_sb[:, t, :], in0=x_sb[:, t, :], in1=tmpd[b][:], op=ALU.add)
                    pend_ssq.append((e2, t))
                    if len(pend_ssq) > 1:
                        t_ssq_last = tile_sumsq(*pend_ssq.pop(0))
                while pend_ssq:
                    t_ssq_last = tile_sumsq(*pend_ssq.pop(0))
                if DBG["nt"] == NT:
                    xstate["rstd_tok"] = norm_stats(have_ss=t_ssq_last)
                barrier()

    def stage_E(layer):
        barrier()
        with ExitStack() as st:
            wo_sb = sb("e_wo", [128, NJ, D], BF16, st)
            hTb = sb("e_hT", [128, 8, 512], BF16, st)
            aT = sb("e_aT", [128, NJ, 512], BF16, st)
            wi = [sb("e_wi%d" % i, [128, 8, 2, 256], BF16, st) for i in range(2)]
            sg = [sb("e_sg%d" % i, [128, 512], F32, st) for i in range(2)]
            tmp = [sb("e_tmp%d" % i, [128, D], F32, st) for i in range(2)]
            hbf = [sb("e_h%d" % i, [128, D], BF16, st) for i in range(2)]
            ps_tr = ps("ps_etr", [128, 8, 128], BF16, st)
            ps_gu = ps("ps_gu", [128, 2, 2, 512], F32, st)
            ps_y = ps("ps_ey", [128, 3, 512], F32, st)
            for j in range(0, NJ, 2):
                dma(pool, s_w[3], [], wo_sb[:, j:j + 2, :], wout[layer, :, j:j + 2, :], max_dma_last_dim=4096)
            t_wo = s_w[3].last()
            t_rstd = norm_stats()
            mslot = mod[:, 1]
            gate = mslot[:, 2, :]
            t_wi_rd = [None, None]
            t_wback = [None, None]
            t_gu_rd = [None, None]
            t_y_rd = [None, None, None]
            t_hT_rd = None
            t_aT_rd = None
            t_tmp = [None, None]
            t_hbf_rd = [None, None]
            t_tr_rd = None
            gi = 0
            ji = 0
            yi = 0
            nblk = DBG["nt"] // 4
            for tb in range(nblk):
                t_hT = None
                for tt in range(4):
                    t = tb * 4 + tt
                    b = t % 2
                    t1 = dve.op([t_rstd, t_tmp[b]], nc.vector.scalar_tensor_tensor, out=tmp[b][:], in0=x_sb[:, t, :],
                                scalar=rstd[:, t:t + 1], in1=mslot[:, 1, :], op0=ALU.mult, op1=ALU.mult)
                    t2 = dve.op([t1, t_hbf_rd[b]], nc.vector.tensor_tensor, out=hbf[b][:], in0=tmp[b][:], in1=mslot[:, 0, :], op=ALU.add)
                    t_tmp[b] = t2
                    t3 = None
                    for kc in range(8):
                        t3 = pe.op([t2, t_tr_rd], nc.tensor.transpose, ps_tr[:, kc, :], hbf[b][:, kc * 128:(kc + 1) * 128], ident[:])
                    t_hbf_rd[b] = t3
                    t_hT = act.op([t3, t_hT_rd], nc.scalar.copy, out=hTb[:, :, tt * 128:(tt + 1) * 128], in_=ps_tr[:])
                    t_tr_rd = t_hT
                t_a_last = None
                for g in range(NG):
                    wb = gi % 2
                    wflat = wi[wb][:].rearrange("p a b c -> p (a b c)")
                    if tb == 0 and not win_cached[layer]:
                        t_w = dma(pool, s_w[wb], [t_wi_rd[wb], t_wback[wb]], wflat,
                                  win[layer, :, g].rearrange("p a b c -> p (a b c)"), max_dma_last_dim=4096)
                        t_wback[wb] = dma(sp, s_wb[wb], [t_w], winb[layer, :, g, :], wflat)
                    else:
                        deps = [t_wi_rd[wb], t_wback[wb]]
                        if tb == 1 and g == 0 and not win_cached[layer]:
                            deps += [s_wb[0].last(), s_wb[1].last()]
                        t_w = dma(sp, s_wl[wb], deps, wflat, winb[layer, :, g, :])
                    gi += 1
                    for jj in range(2):
                        j = 2 * g + jj
                        slot = ji % 2
                        ji += 1
                        tmm = [None, None]
                        for gu in range(2):
                            for kc in range(8):
                                tmm[gu] = pe.op([t_w, t_hT, t_gu_rd[slot], t_aT_rd], nc.tensor.matmul, ps_gu[:, slot, gu, :],
                                                lhsT=wi[wb][:, kc, gu, jj * 128:(jj + 1) * 128], rhs=hTb[:, kc, :],
                                                start=(kc == 0), stop=(kc == 7))
                        t_wi_rd[wb] = tmm[1]
                        ts = act.op([tmm[0]], nc.scalar.activation, out=sg[slot][:], in_=ps_gu[:, slot, 0, :], func=AF.Silu)
                        ta = dve.op([ts, tmm[1]], nc.vector.tensor_tensor, out=aT[:, j, :], in0=ps_gu[:, slot, 1, :], in1=sg[slot][:], op=ALU.mult)
                        t_gu_rd[slot] = ta
                        t_a_last = ta
                t_hT_rd = t_wi_rd[(gi - 1) % 2]
                t_last_out = None
                for tt in range(4):
                    t = tb * 4 + tt
                    for cb in range(2):
                        ys = yi % 3
                        yi += 1
                        tm = None
                        for j in range(NJ):
                            tm = pe.op([t_a_last, t_wo, t_y_rd[ys]], nc.tensor.matmul, ps_y[:, ys, :], lhsT=aT[:, j, tt * 128:(tt + 1) * 128],
                                       rhs=wo_sb[:, j, cb * 512:(cb + 1) * 512], start=(j == 0), stop=(j == NJ - 1))
                        t_last_out = tm
                        b = yi % 2
                        e1 = dve.op([tm], nc.vector.tensor_tensor, out=tmp[b][:, 0:512], in0=ps_y[:, ys, :],
                                    in1=gate[:, cb * 512:(cb + 1) * 512], op=ALU.mult)
                        t_y_rd[ys] = e1
                        t_tmp[b] = dve.op([e1], nc.vector.tensor_tensor, out=x_sb[:, t, cb * 512:(cb + 1) * 512],
                                          in0=x_sb[:, t, cb * 512:(cb + 1) * 512], in1=tmp[b][:, 0:512], op=ALU.add)
                        if cb == 1:
                            t_ssq_last = tile_sumsq(t_tmp[b], t)
                t_aT_rd = t_last_out
            win_cached[layer] = True
            if DBG["nt"] == NT:
                xstate["rstd_tok"] = norm_stats(have_ss=t_ssq_last)
            barrier()

    def final_norm():
        barrier()
        with ExitStack() as st:
            gB = sb("f_g", [128, D], F32, st)
            ob = [sb("f_o%d" % i, [128, D], F32, st) for i in range(2)]
            s_f = [Slot(ctx, "f%d" % i) for i in range(2)]
            t_g = dma(sp, s_c, [], gB[:], final_g.ap.rearrange("a b -> (a b)").partition_broadcast(128))
            t_rstd = norm_stats()
            outv = x_out.rearrange("(t p) d -> p t d", p=128)
            t_st = [None, None]
            for t in range(NT):
                b = t % 2
                t1 = dve.op([t_rstd, t_g, t_st[b]], nc.vector.scalar_tensor_tensor, out=ob[b][:], in0=x_sb[:, t, :],
                            scalar=rstd[:, t:t + 1], in1=gB[:], op0=ALU.mult, op1=ALU.mult)
                t_st[b] = dma(sp, s_f[b], [t1], outv[:, t, :], ob[b][:])
            barrier()

    def store_x():
        barrier()
        s_f = Slot(ctx, "xs")
        outv = x_out.rearrange("(t p) d -> p t d", p=128)
        for g in range(4):
            dma(sp, s_f, [], outv[:, 4 * g:4 * g + 4, :], x_sb[:, 4 * g:4 * g + 4, :])
        barrier()

    def adaln_all():
        for idx, (layer, half, gain) in enumerate(((0, 0, norm1_g[0]), (0, 1, norm2_g[0]), (1, 0, norm1_g[1]), (1, 1, norm2_g[1]))):
            adaln(layer, half, gain)
            dma(sp, s_m, [], modsave[idx:idx + 1, :], mod[0:1, half].rearrange("p a b -> p (a b)"))
        barrier()

    def load_mod(idx, half):
        barrier()
        dma(sp, s_m, [], mod[:, half].rearrange("p a b -> p (a b)"), modsave[idx].partition_broadcast(128))
        barrier()

    def exchange(layer):
        barrier()
        kvo = kvo0 if layer == 0 else kvo1
        kva = kva0 if layer == 0 else kva1
        s_cc = Slot(ctx, "cc%d" % layer)
        for d_ in []:
            pass
        pool.h.collective_compute("AllGather", ALU.bypass, replica_groups=[[0, 1], [2, 3], [4, 5], [6, 7]],
                                  ins=[kvo], outs=[kva]).then_inc(s_cc.sem, 16)
        s_cc.cnt += 16
        barrier()

    if fused:
        adaln_all()
        ada_stack.close()
        load_mod(0, 0)
        stage_A(0, qd=q0, kvo=kva0[2048:4096], rope=ropeA.ap)
        barrier()
        load_x(x_oth)
        stage_A(0, qd=q0b, kvo=kva0[0:2048], rope=ropeA_o.ap)
        stage_CD(0, qd=q0b)
        load_mod(1, 1)
        stage_E(0)
        load_mod(2, 0)
        stage_A(1, qd=q1, kvo=kva1[0:512], rope=ropeB_o.ap, need_q=False)
        barrier()
        load_x(x_in)
        load_mod(0, 0)
        stage_CD(0, qd=q0)
        stage_E(0)
        load_mod(2, 0)
        stage_A(1, qd=q1, kvo=kva1[512:1024], rope=ropeB.ap)
        stage_CD(1, qd=q1)
        load_mod(3, 1)
        stage_E(1)
        final_norm()
        phases = []
    for ph in phases:
        if ph == "A0":
            adaln(0, 0, norm1_g[0])
            stage_A(0)
        elif ph == "A1":
            adaln(1, 0, norm1_g[1])
            stage_A(1)
            if last == "A1":
                store_x()
        elif ph == "C0":
            if "A0" not in phases:
                adaln(0, 0, norm1_g[0])
            if "CD" not in DBG.get("skip", ()):
                stage_CD(0)
            adaln(0, 1, norm2_g[0])
            if "E" not in DBG.get("skip", ()):
                stage_E(0)
        elif ph == "C1":
            if "A1" not in phases:
                adaln(1, 0, norm1_g[1])
            if "CD" not in DBG.get("skip", ()):
                stage_CD(1)
            adaln(1, 1, norm2_g[1])
            if "E" not in DBG.get("skip", ()):
                stage_E(1)
            final_norm()
        elif ph == "DBG_ADA":
            adaln(0, 0, norm1_g[0])
            dbg = nc.dram_tensor("dbg", [128, 3 * D], F32, kind="ExternalOutput").ap()
            dma(sp, s_m, [], dbg, mod[:, 0].rearrange("p a b -> p (a b)"))
        elif ph == "DBG_INIT":
            barrier()
            dbg = nc.dram_tensor("dbg", [128, 8 * 128], BF16, kind="ExternalOutput").ap()
            dma(sp, s_m, [], dbg, condB[:].rearrange("p a b -> p (a b)"))
        else:
            raise NotImplementedError(ph)

    barrier()
    if not fused:
        ada_stack.close()
    es.close()
    return nc, used_inputs


def _bf16(a):
    return np.ascontiguousarray(a).astype(ml_dtypes.bfloat16)


def rope_tables():
    t = np.arange(SEQ, dtype=np.int32)
    rows = SEQ // 64
    row_pos = np.broadcast_to(np.arange(rows, dtype=np.int32)[:, None], (rows, 64)).reshape(SEQ)
    col_pos = np.broadcast_to(np.arange(64, dtype=np.int32)[None, :], (rows, 64)).reshape(SEQ)

    def ang(pos, dim, theta):
        half = dim // 2
        freqs = (np.float32(theta) ** (-np.arange(half, dtype=np.float32) / np.float32(half))).astype(np.float32)
        a = pos.astype(np.float32)[:, None] * freqs[None, :]
        return np.cos(a).astype(np.float32), np.sin(a).astype(np.float32)

    ca, sa = ang(t, 16, 500000.0)
    cr, sr = ang(row_pos, 64, 10000.0)
    cc, sc = ang(col_pos, 64, 10000.0)
    A = np.stack([ca, sa], axis=1)
    B = np.stack([cr, sr, cc, sc], axis=1)
    return A, B


def common_inputs(inp):
    f = lambda a: np.ascontiguousarray(np.asarray(a, dtype=np.float32))
    m = {}
    m["ident"] = np.eye(128, dtype=np.float32).astype(ml_dtypes.bfloat16)
    aw = f(inp["ada_w"]).reshape(2, 8, 128, 6, 1024).transpose(0, 2, 3, 1, 4)
    m["ada_w"] = np.ascontiguousarray(aw)
    m["ada_b"] = f(inp["ada_b"])
    m["norm1_g"] = f(inp["norm1_g"])
    m["norm2_g"] = f(inp["norm2_g"])
    m["wqkv0"] = np.ascontiguousarray(f(inp["a_w_qkv"])[0].reshape(8, 128, 3072).transpose(1, 0, 2))
    m["wo0"] = np.ascontiguousarray(f(inp["a_w_o"])[0].reshape(8, 128, 1024).transpose(1, 0, 2))
    m["lamp"] = np.ascontiguousarray(np.concatenate([f(inp["a_lam_q1"]), f(inp["a_lam_q2"]), f(inp["a_lam_k1"]), f(inp["a_lam_k2"])], axis=0))
    m["subln_g"] = f(inp["a_subln_g"])
    m["wqkv1"] = np.ascontiguousarray(f(inp["b_w_qkv"])[0].reshape(8, 128, 1536).transpose(1, 0, 2))
    m["wo1"] = np.ascontiguousarray(f(inp["b_w_o"])[0].reshape(8, 128, 1024).transpose(1, 0, 2))
    m["qk_g"] = np.ascontiguousarray(np.concatenate([f(inp["b_qnorm_g"]), f(inp["b_knorm_g"])], axis=0))
    wi = f(inp["f_w_in"]).reshape(2, 8, 128, 2, NG, 256).transpose(0, 2, 4, 1, 3, 5)
    m["win"] = np.ascontiguousarray(wi)
    wo = f(inp["f_w_out"]).reshape(2, NJ, 128, 1024).transpose(0, 2, 1, 3)
    m["wout"] = np.ascontiguousarray(wo)
    m["final_g"] = f(inp["final_g"]).reshape(1, D)
    return m


def core_inputs(inp, core, A, B):
    b, r = divmod(core, 2)
    x = np.asarray(inp["x"], dtype=np.float32)
    c = np.asarray(inp["c"], dtype=np.float32)
    m = {}
    m["x_in"] = np.ascontiguousarray(x[b, r * TOWN:(r + 1) * TOWN])
    m["cT"] = np.ascontiguousarray(c[b].reshape(8, 128).T)
    sl = slice(r * TOWN, (r + 1) * TOWN)
    m["ropeA"] = np.ascontiguousarray(A[sl].reshape(NT, 128, 2, 8).transpose(1, 0, 2, 3))
    m["ropeB"] = np.ascontiguousarray(B[sl].reshape(NT, 128, 4, 32).transpose(1, 0, 2, 3))
    return m


_PROG_CACHE = {}


def _get_prog(phases, fused):
    key = (tuple(phases), fused)
    if key not in _PROG_CACHE:
        _PROG_CACHE[key] = build(list(phases), fused)
    return _PROG_CACHE[key]


def _run(phases, fused, per_core):
    import time as _t
    t0 = _t.time()
    nc, used = _get_prog(phases, fused)
    in_maps = [{k: m[k] for k in used} for m in per_core]
    t1 = _t.time()
    res = run_bass_kernel_spmd(nc, in_maps, core_ids=list(range(NCORES)))
    print("[kernel] launch %s build %.1fs run %.1fs" % (phases, t1 - t0, _t.time() - t1), flush=True)
    return res.results


def kernel_unfused(**inp):
    A, B = rope_tables()
    cm = common_inputs(inp)
    per_core = []
    for core in range(NCORES):
        m = dict(cm)
        m.update(core_inputs(inp, core, A, B))
        per_core.append(m)
    r1 = _run(["A0"], False, per_core)
    for core in range(NCORES):
        b = core // 2
        per_core[core]["q0"] = r1[core]["q0"]
        per_core[core]["kva0"] = np.concatenate([r1[2 * b]["kvo0"], r1[2 * b + 1]["kvo0"]], axis=0)
    r2 = _run(["C0", "A1"], False, per_core)
    for core in range(NCORES):
        b = core // 2
        per_core[core]["q1"] = r2[core]["q1"]
        per_core[core]["kva1"] = np.concatenate([r2[2 * b]["kvo1"], r2[2 * b + 1]["kvo1"]], axis=0)
        per_core[core]["x_in"] = r2[core]["x_mid"]
    r3 = _run(["C1"], False, per_core)
    out = np.empty((NB, SEQ, D), dtype=np.float32)
    for core in range(NCORES):
        b, r = divmod(core, 2)
        out[b, r * TOWN:(r + 1) * TOWN] = np.asarray(r3[core]["out"], dtype=np.float32)
    return out


def kernel_fused(**inp):
    A, B = rope_tables()
    cm = common_inputs(inp)
    x = np.asarray(inp["x"], dtype=np.float32)
    per_core = []
    for core in range(NCORES):
        b, r = divmod(core, 2)
        m = dict(cm)
        m.update(core_inputs(inp, core, A, B))
        o = 1 - r
        sl = slice(o * TOWN, (o + 1) * TOWN)
        m["x_oth"] = np.ascontiguousarray(x[b, sl])
        m["ropeA_o"] = np.ascontiguousarray(A[sl].reshape(NT, 128, 2, 8).transpose(1, 0, 2, 3))
        m["ropeB_o"] = np.ascontiguousarray(B[sl].reshape(NT, 128, 4, 32).transpose(1, 0, 2, 3))
        per_core.append(m)
    r = _run(["A0", "C0", "A1", "C1"], True, per_core)
    out = np.empty((NB, SEQ, D), dtype=np.float32)
    for core in range(NCORES):
        b, rr = divmod(core, 2)
        out[b, rr * TOWN:(rr + 1) * TOWN] = np.asarray(r[core]["out"], dtype=np.float32)
    return out


def kernel(**inp):
    return kernel_fused(**inp)
```

```python
import math
from contextlib import ExitStack

import numpy as np
import ml_dtypes
import concourse.bass as bass
import concourse.mybir as mybir
from concourse.bass_utils import run_bass_kernel_spmd

F32 = mybir.dt.float32
BF16 = mybir.dt.bfloat16
AF = mybir.ActivationFunctionType
ALU = mybir.AluOpType
AX = mybir.AxisListType

D = 1024
SEQ = 4096
NB = 4
NCORES = 8
TOWN = 2048
NT = 16
DFF = 2816
NJ = 22
NG = 11
EPS = 1e-6
VW = 130
KVW = 16 * VW
DBG = {"lvl": 99, "nt": NT}
SAME_SYNC = True


def lambda_init_fn(layer_idx):
    return 0.8 - 0.6 * math.exp(-0.3 * layer_idx)


class Ctx:
    def __init__(self, nc, es):
        self.nc = nc
        self.es = es
        self.nsem = 0
        self.slots = []

    def new_sem(self, name):
        self.nsem += 1
        return self.es.enter_context(self.nc.semaphore(name))


class Engine:
    def __init__(self, ctx, name, h, counts=True):
        self.h = h
        self.name = name
        self.chain = name in ("act", "dve", "pool")
        self.sem = ctx.new_sem("e_" + name) if counts else None
        self.cnt = 0
        self.seen = {}

    def wait(self, tok):
        if tok is None:
            return
        sem, val, key = tok
        if val <= 0:
            return
        if key == self.name and (self.name == "pe" or not SAME_SYNC):
            return
        if self.seen.get(key, 0) >= val:
            return
        self.seen[key] = val
        self.h.wait_ge(sem, val)

    def op(self, deps, fn, *args, chain=None, **kw):
        for d in deps:
            self.wait(d)
        if (self.chain if chain is None else chain) and self.cnt > 0:
            self.wait((self.sem, self.cnt, self.name))
        ins = fn(*args, **kw)
        self.cnt += 1
        ins.then_inc(self.sem, 1)
        return (self.sem, self.cnt, self.name)

    def op_nomark(self, deps, fn, *args, **kw):
        for d in deps:
            self.wait(d)
        fn(*args, **kw)
        return None

    def last(self):
        if self.sem is None:
            return None
        return (self.sem, self.cnt, self.name)


class Slot:
    def __init__(self, ctx, name):
        name = "%s_%d" % (name, len(ctx.slots))
        self.sem = ctx.new_sem("s_" + name)
        self.cnt = 0
        self.key = "s_" + name
        ctx.slots.append(self)

    def last(self):
        return (self.sem, self.cnt, self.key)


def dma(q, slot, deps, out, in_, **kw):
    for d in deps:
        q.wait(d)
    q.h.dma_start(out=out, in_=in_, **kw).then_inc(slot.sem, 16)
    slot.cnt += 16
    return (slot.sem, slot.cnt, slot.key)


def build(phases, fused):
    nc = bass.Bass("TRN2", target_bir_lowering=False)
    es = ExitStack()
    ctx = Ctx(nc, es)

    class LazyIn:
        def __init__(self, name, shape, dt):
            self.name, self.shape, self.dt, self._ap = name, shape, dt, None

        @property
        def ap(self):
            if self._ap is None:
                self._ap = nc.dram_tensor(self.name, list(self.shape), self.dt, kind="ExternalInput").ap()
                used_inputs.append(self.name)
            return self._ap

        def __getitem__(self, k):
            return self.ap[k]

        def rearrange(self, *a, **k):
            return self.ap.rearrange(*a, **k)

        def partition_broadcast(self, n):
            return self.ap.partition_broadcast(n)

    used_inputs = []

    def dram(name, shape, dt, kind):
        if kind == "ExternalInput":
            return LazyIn(name, shape, dt)
        return nc.dram_tensor(name, list(shape), dt, kind=kind).ap()

    first = phases[0]
    last = phases[-1]
    x_in = dram("x_in", [TOWN, D], F32, "ExternalInput")
    cT = dram("cT", [128, 8], F32, "ExternalInput")
    ident_d = dram("ident", [128, 128], BF16, "ExternalInput")
    ada_w = dram("ada_w", [2, 128, 6, 8, 1024], F32, "ExternalInput")
    ada_b = dram("ada_b", [2, 6144], F32, "ExternalInput")
    norm1_g = dram("norm1_g", [2, D], F32, "ExternalInput")
    norm2_g = dram("norm2_g", [2, D], F32, "ExternalInput")
    wqkv0 = dram("wqkv0", [128, 8, 3072], F32, "ExternalInput")
    wo0 = dram("wo0", [128, 8, 1024], F32, "ExternalInput")
    lamp = dram("lamp", [4, 64], F32, "ExternalInput")
    subln_g = dram("subln_g", [1, 128], F32, "ExternalInput")
    wqkv1 = dram("wqkv1", [128, 8, 1536], F32, "ExternalInput")
    wo1 = dram("wo1", [128, 8, 1024], F32, "ExternalInput")
    qk_g = dram("qk_g", [2, 128], F32, "ExternalInput")
    win = dram("win", [2, 128, NG, 8, 2, 256], F32, "ExternalInput")
    wout = dram("wout", [2, 128, NJ, 1024], F32, "ExternalInput")
    final_g = dram("final_g", [1, D], F32, "ExternalInput")
    ropeA = dram("ropeA", [128, NT, 2, 8], F32, "ExternalInput")
    ropeB = dram("ropeB", [128, NT, 4, 32], F32, "ExternalInput")

    def scratch(name, shape, produced_by, consumed_by):
        if fused:
            return dram(name, shape, BF16, "Internal")
        if produced_by in phases and consumed_by in phases:
            return dram(name, shape, BF16, "Internal")
        if produced_by in phases:
            return dram(name, shape, BF16, "ExternalOutput")
        if consumed_by in phases:
            return dram(name, shape, BF16, "ExternalInput")
        return None

    q0 = scratch("q0", [8, 128, TOWN], "A0", "C0")
    kvo0 = scratch("kvo0", [2048, KVW], "A0", None)
    q1 = scratch("q1", [8, 128, TOWN], "A1", "C1")
    kvo1 = scratch("kvo1", [512, KVW], "A1", None)
    if fused:
        kva0 = dram("kva0", [2 * 2048, KVW], BF16, "Internal")
        kva1 = dram("kva1", [2 * 512, KVW], BF16, "Internal")
        q0b = dram("q0b", [8, 128, TOWN], BF16, "Internal")
        x_oth = dram("x_oth", [TOWN, D], F32, "ExternalInput")
        modsave = dram("modsave", [4, 3 * D], F32, "Internal")
        ropeA_o = dram("ropeA_o", [128, NT, 2, 8], F32, "ExternalInput")
        ropeB_o = dram("ropeB_o", [128, NT, 4, 32], F32, "ExternalInput")
    else:
        kva0 = dram("kva0", [2 * 2048, KVW], BF16, "ExternalInput") if "C0" in phases else None
        kva1 = dram("kva1", [2 * 512, KVW], BF16, "ExternalInput") if "C1" in phases else None
    x_out = None
    if last == "C1" or fused:
        x_out = dram("out", [TOWN, D], F32, "ExternalOutput")
    elif not fused and last == "A1":
        x_out = dram("x_mid", [TOWN, D], F32, "ExternalOutput")

    pe = Engine(ctx, "pe", nc.tensor)
    act = Engine(ctx, "act", nc.scalar)
    dve = Engine(ctx, "dve", nc.vector)
    pool = Engine(ctx, "pool", nc.gpsimd)
    sp = Engine(ctx, "sp", nc.sync, counts=False)
    engines = [pe, act, dve, pool]

    def barrier():
        toks = [e.last() for e in engines] + [s.last() for s in ctx.slots]
        for e in engines + [sp]:
            for t in toks:
                e.wait(t)

    uniq = [0]

    def sb(name, shape, dt, stack=es):
        uniq[0] += 1
        return stack.enter_context(nc.sbuf_tensor("%s_%d" % (name, uniq[0]), list(shape), dt))

    def ps(name, shape, dt, stack):
        uniq[0] += 1
        return stack.enter_context(nc.psum_tensor("%s_%d" % (name, uniq[0]), list(shape), dt))

    x_sb = sb("x_sb", [128, NT, D], F32)
    mod = sb("mod", [128, 2, 3, D], F32)
    ident = sb("ident_sb", [128, 128], BF16)
    ss = sb("ss", [128, NT], F32)
    rstd = sb("rstd", [128, NT], F32)
    lam_t = sb("lam_t", [128, 4], F32)
    junk = sb("junk", [128, D], BF16)
    ada_stack = ExitStack()
    condB = sb("condB", [128, 8, 128], BF16, ada_stack)
    cond = sb("cond", [128, 8], F32, ada_stack)
    ones_f = sb("ones_f", [128, 128], F32, ada_stack)

    s_x = Slot(ctx, "x")
    s_c = Slot(ctx, "c")
    s_w = [Slot(ctx, "w%d" % i) for i in range(4)]
    s_o = [Slot(ctx, "o%d" % i) for i in range(3)]
    s_ov = [Slot(ctx, "ov%d" % i) for i in range(2)]
    s_m = Slot(ctx, "m")
    s_wb = [Slot(ctx, "wb%d" % i) for i in range(2)]
    s_wl = [Slot(ctx, "wl%d" % i) for i in range(2)]
    winb = nc.dram_tensor("winb", [2, 128, NG, 4096], BF16, kind="Internal").ap()
    win_cached = {0: False, 1: False}

    xstate = {}

    def load_x(src):
        xv = src.rearrange("(t p) d -> p t d", p=128)
        for g in range(4):
            dma(sp, s_x, [], x_sb[:, 4 * g:4 * g + 4, :], xv[:, 4 * g:4 * g + 4, :])
        xstate["t"] = s_x.last()
        xstate["rstd_tok"] = None

    load_x(x_in)
    dma(sp, s_c, [], ident[:], ident_d.ap)
    dma(sp, s_c, [], cond[:], cT.ap)
    t_c = s_c.last()
    t_ones = dve.op([], nc.vector.memset, ones_f[:], 1.0)
    t_cond = act.op([t_c], nc.scalar.activation, out=cond[:], in_=cond[:], func=AF.Silu)
    t_cb = None
    for kc in range(8):
        t_cb = dve.op([t_cond, t_ones], nc.vector.tensor_scalar, out=condB[:, kc, :], in0=ones_f[:],
                      scalar1=cond[:, kc:kc + 1], scalar2=None, op0=ALU.mult)

    def adaln(layer, half, gain_dram):
        barrier()
        with ExitStack() as st:
            wb = [sb("adaw%d" % i, [128, 8, 1024], BF16, st) for i in range(2)]
            gB = sb("gB", [128, D], F32, st)
            pacc = ps("ps_ada", [128, 2, 512], F32, st)
            mslot = mod[:, half]
            for j in range(3):
                dma(sp, s_m, [], mslot[:, j, :],
                    ada_b[layer, (half * 3 + j) * D:(half * 3 + j + 1) * D].partition_broadcast(128))
            dma(sp, s_m, [], gB[:], gain_dram.partition_broadcast(128))
            t_m = s_m.last()
            t_add = None
            for j in range(3):
                blk = half * 3 + j
                t_w = dma(pool, s_w[j % 2], [t_add] if j >= 2 else [], wb[j % 2][:], ada_w[layer, :, blk],
                          max_dma_last_dim=8192)
                for cb in range(2):
                    t_mm = None
                    for kc in range(8):
                        t_mm = pe.op([t_w, t_cb, t_add], nc.tensor.matmul, pacc[:, cb, :], lhsT=condB[:, kc, :],
                                     rhs=wb[j % 2][:, kc, cb * 512:(cb + 1) * 512], start=(kc == 0), stop=(kc == 7))
                t_add = dve.op([t_mm, t_m], nc.vector.tensor_tensor, out=mslot[:, j, :], in0=pacc[:].rearrange("p a b -> p (a b)"),
                               in1=mslot[:, j, :], op=ALU.add)
            dve.op([t_add], nc.vector.scalar_tensor_tensor, out=mslot[:, 1, :], in0=mslot[:, 1, :], scalar=1.0,
                   in1=gB[:], op0=ALU.add, op1=ALU.mult)
        barrier()

    def tile_sumsq(dep, i):
        return act.op([dep], nc.scalar.activation, out=junk[:], in_=x_sb[:, i, :], func=AF.Square,
                      accum_out=ss[:, i:i + 1])

    def norm_stats(ntiles=NT, t0=0, have_ss=None):
        if have_ss is None and xstate.get("rstd_tok") is not None:
            t = xstate["rstd_tok"]
            xstate["rstd_tok"] = None
            return t
        t = have_ss
        if have_ss is None:
            for i in range(t0, t0 + ntiles):
                t = tile_sumsq(xstate["t"], i)
        sl = slice(t0, t0 + ntiles)
        t = dve.op([t], nc.vector.tensor_scalar, out=rstd[:, sl], in0=ss[:, sl], scalar1=1.0 / D, scalar2=EPS,
                   op0=ALU.mult, op1=ALU.add)
        t = act.op([t], nc.scalar.activation, out=rstd[:, sl], in_=rstd[:, sl], func=AF.Ln)
        t = act.op([t], nc.scalar.activation, out=rstd[:, sl], in_=rstd[:, sl], func=AF.Exp, scale=-0.5)
        return t

    def stage_A(layer, qd=None, kvo=None, rope=None, need_q=True):
        ncols = 3072 if layer == 0 else 1536
        wq = wqkv0 if layer == 0 else wqkv1
        if qd is None:
            qd = q0 if layer == 0 else q1
        if kvo is None:
            kvo = kvo0 if layer == 0 else kvo1
        if rope is None:
            rope = (ropeA if layer == 0 else ropeB).ap
        nkc = 8 if layer == 0 else 2
        cbs = list(range(ncols // 512))
        if not need_q:
            assert layer == 1
            cbs = [2]
        barrier()
        with ExitStack() as st:
            W = sb("wqkv_sb", [128, 8, ncols], BF16, st)
            tmp = [sb("a_tmp%d" % i, [128, D], F32, st) for i in range(2)]
            hbf = [sb("a_h%d" % i, [128, D], BF16, st) for i in range(2)]
            hT = [sb("a_hT%d" % i, [128, 8, 128], BF16, st) for i in range(2)]
            qkbf = [sb("a_qk%d" % i, [128, 1024 + nkc * 128], BF16, st) for i in range(2)]
            qst = sb("a_qst", [128, 8, 512], BF16, st)
            kst = sb("a_kst", [128, nkc, 512], BF16, st)
            vst2 = [sb("a_vst%d" % i, [128, nkc, 4, VW], BF16, st) for i in range(2)]
            rp = sb("a_rope", [128, NT, 2, 8] if layer == 0 else [128, NT, 4, 32], F32, st)
            rt = [sb("a_rt%d" % i, [128, 16, 8] if layer == 0 else [128, 10, 2, 32], F32, st) for i in range(4)]
            ps_h = ps("ps_h", [128, 8, 128], BF16, st)
            ps_t = ps("ps_t", [128, 8, 128], BF16, st)
            ps_q = ps("ps_qkv", [128, 6, 512], F32, st)
            if layer == 1:
                gq = sb("a_gq", [128, 2, 128], F32, st)
                sq2 = [sb("a_sq%d" % i, [128, 1280], F32, st) for i in range(2)]
                ssq = sb("a_ssq", [128, 10], F32, st)
                rq = sb("a_rq", [128, 10], F32, st)

            kcs = range(8) if need_q else [k_ for k_ in range(8)]
            for kc in range(8):
                if need_q:
                    dma(pool, s_w[2], [], W[:, kc, :], wq[:, kc, :], max_dma_last_dim=4096)
                else:
                    dma(pool, s_w[2], [], W[:, kc, 1024:1536], wq[:, kc, 1024:1536], max_dma_last_dim=2048)
            t_W = s_w[2].last()
            dma(sp, s_c, [], rp[:], rope)
            if layer == 1:
                dma(sp, s_c, [], gq[:].rearrange("p a b -> p (a b)"),
                    qk_g.ap.rearrange("a b -> (a b)").partition_broadcast(128))
            t_rp = s_c.last()
            dve.op([], nc.vector.memset, vst2[0][:, :, :, 128:130], 1.0)
            t_ones_v = dve.op([], nc.vector.memset, vst2[1][:, :, :, 128:130], 1.0)
            t_rstd = norm_stats()
            mslot = mod[:, 0]
            st_out_v = [None, None]

            S = {"tmp": [None, None], "hbf_rd": [None, None], "psh_rd": None, "hT_rd": [None, None],
                 "grp_rd": {}, "qkT_rd": [None, None], "pst_rd": None, "st_out": [None, None, None],
                 "qk_ready": {}, "tv": {}}

            two_sets = (layer == 1)
            t4s = {}

            def psq(t, cb):
                return ps_q[:, (3 * (t % 2) + cb) if two_sets else cb, :]

            def norm(t):
                b = t % 2
                t1 = dve.op([t_rstd, S["tmp"][b]], nc.vector.scalar_tensor_tensor, out=tmp[b][:], in0=x_sb[:, t, :],
                            scalar=rstd[:, t:t + 1], in1=mslot[:, 1, :], op0=ALU.mult, op1=ALU.mult)
                if layer == 0:
                    t2 = pool.op([t1, S["hbf_rd"][b]], nc.gpsimd.tensor_tensor, out=hbf[b][:], in0=tmp[b][:],
                                 in1=mslot[:, 0, :], op=ALU.add)
                else:
                    t2 = dve.op([t1, S["hbf_rd"][b]], nc.vector.tensor_tensor, out=hbf[b][:], in0=tmp[b][:],
                                in1=mslot[:, 0, :], op=ALU.add)
                S["tmp"][b] = t2

            def htr(t):
                b = t % 2
                t3 = None
                for kc in range(8):
                    t3 = pe.op([S["tmp"][b], S["psh_rd"]], nc.tensor.transpose, ps_h[:, kc, :],
                               hbf[b][:, kc * 128:(kc + 1) * 128], ident[:])
                S["hbf_rd"][b] = t3
                t4 = act.op([t3, S["hT_rd"][b]], nc.scalar.copy, out=hT[b][:], in_=ps_h[:])
                S["psh_rd"] = t4
                t4s[t] = t4

            t_mm = {}

            def mm(t, cb):
                b = t % 2
                if layer == 0:
                    grp = cb // 2
                else:
                    grp = t % 2
                tm = None
                for kc in range(8):
                    tm = pe.op([t4s[t], t_W] + S["grp_rd"].get(grp, []), nc.tensor.matmul, psq(t, cb), lhsT=hT[b][:, kc, :],
                               rhs=W[:, kc, cb * 512:(cb + 1) * 512], start=(kc == 0), stop=(kc == 7))
                t_mm[(t, cb)] = tm
                S["hT_rd"][b] = tm

            def post_qk(t, qi):
                b = t % 2
                tcp = act.op([t_mm[(t, 2 * qi + 1)], S["qkT_rd"][b]], nc.scalar.copy, out=qkbf[b][:, qi * 1024:(qi + 1) * 1024],
                             in_=ps_q[:, 2 * qi:2 * qi + 2, :].rearrange("p a b -> p (a b)"))
                cosb = rp[:, t, 0:1, :].to_broadcast([128, 16, 8])
                sinb = rp[:, t, 1:2, :].to_broadcast([128, 16, 8])
                P = ps_q[:, 2 * qi:2 * qi + 2, :].rearrange("p a (h d) -> p (a h) d", d=64)
                x1 = P[:, :, 0:8]
                x2 = P[:, :, 8:16]
                O = qkbf[b][:, qi * 1024:(qi + 1) * 1024].rearrange("p (h d) -> p h d", d=64)
                a1 = dve.op([t_rp, tcp], nc.vector.tensor_tensor, out=rt[0][:], in0=x1, in1=cosb, op=ALU.mult)
                a2 = dve.op([a1], nc.vector.tensor_tensor, out=rt[1][:], in0=x2, in1=sinb, op=ALU.mult)
                a4 = dve.op([a2], nc.vector.tensor_tensor, out=rt[2][:], in0=x2, in1=cosb, op=ALU.mult)
                a5 = dve.op([a4], nc.vector.tensor_tensor, out=rt[3][:], in0=x1, in1=sinb, op=ALU.mult)
                S["grp_rd"][qi] = [a5]
                a3 = dve.op([a5], nc.vector.tensor_tensor, out=O[:, :, 0:8], in0=rt[0][:], in1=rt[1][:], op=ALU.subtract)
                a6 = dve.op([a3], nc.vector.tensor_tensor, out=O[:, :, 8:16], in0=rt[2][:], in1=rt[3][:], op=ALU.add)
                S["qk_ready"].setdefault(t, []).append(a6)

            def post_v0(t):
                g4_, tt = divmod(t, 4)
                vst = vst2[g4_ % 2]
                tv = dve.op([t_mm[(t, 5)], st_out_v[g4_ % 2] if tt == 0 else None], nc.vector.tensor_copy, out=vst[:, :, tt, 0:128],
                            in_=ps_q[:, 4:6, :].rearrange("p a (h d) -> p (a h) d", d=128))
                S["grp_rd"][2] = [tv]
                S["tv"][t] = tv

            def post1(t):
                b = t % 2
                g4_, tt = divmod(t, 4)
                vst = vst2[g4_ % 2]
                base = 3 * (t % 2)
                PQ = ps_q[:, base:base + 3, :].rearrange("p a b -> p (a b)")
                lo = 0 if need_q else 1024
                h0 = lo // 128
                nh = 10 - h0
                sq = sq2[t % 2]
                b1 = act.op([t_mm[(t, 2)]] + S.get(("sq_rd", t % 2), []), nc.scalar.activation, out=sq[:, lo:1280], in_=PQ[:, lo:1280], func=AF.Square)
                b2 = dve.op([b1], nc.vector.tensor_reduce, out=ssq[:, h0:10], in_=sq[:, lo:1280].rearrange("p (h d) -> p h d", d=128),
                            axis=AX.X, op=ALU.add)
                b3 = dve.op([b2], nc.vector.tensor_scalar, out=rq[:, h0:10], in0=ssq[:, h0:10], scalar1=1.0 / 128, scalar2=EPS,
                            op0=ALU.mult, op1=ALU.add)
                b4 = act.op([b3], nc.scalar.activation, out=rq[:, h0:10], in_=rq[:, h0:10], func=AF.Ln)
                b5 = act.op([b4], nc.scalar.activation, out=rq[:, h0:10], in_=rq[:, h0:10], func=AF.Exp, scale=-0.5)
                b6 = dve.op([b5], nc.vector.tensor_tensor, out=sq[:, lo:1280].rearrange("p (h d) -> p h d", d=128),
                            in0=PQ[:, lo:1280].rearrange("p (h d) -> p h d", d=128),
                            in1=rq[:, h0:10].unsqueeze(2).to_broadcast([128, nh, 128]), op=ALU.mult)
                tv = act.op([t_mm[(t, 2)], b6, st_out_v[g4_ % 2] if tt == 0 else None], nc.scalar.copy, out=vst[:, :, tt, 0:128],
                            in_=PQ[:, 1280:1536].rearrange("p (h d) -> p h d", d=128))
                S["grp_rd"][t % 2] = [b6, tv]
                b7 = b6
                if need_q:
                    b7 = dve.op([b6, t_rp], nc.vector.tensor_tensor, out=sq[:, 0:1024].rearrange("p (h d) -> p h d", d=128),
                                in0=sq[:, 0:1024].rearrange("p (h d) -> p h d", d=128),
                                in1=gq[:, 0:1, :].to_broadcast([128, 8, 128]), op=ALU.mult)
                b8 = dve.op([b7, t_rp], nc.vector.tensor_tensor, out=sq[:, 1024:1280].rearrange("p (h d) -> p h d", d=128),
                            in0=sq[:, 1024:1280].rearrange("p (h d) -> p h d", d=128),
                            in1=gq[:, 1:2, :].to_broadcast([128, 2, 128]), op=ALU.mult)
                X = sq[:].rearrange("p (h a b d) -> p h a b d", a=2, b=2, d=32)
                O = qkbf[b][:].rearrange("p (h a b d) -> p h a b d", a=2, b=2, d=32)
                tab = rp[:, t].rearrange("p (a c) d -> p a c d", c=2)
                cosb = tab[:, :, 0, :].unsqueeze(1).to_broadcast([128, nh, 2, 32])
                sinb = tab[:, :, 1, :].unsqueeze(1).to_broadcast([128, nh, 2, 32])
                x1 = X[:, h0:10, :, 0, :]
                x2 = X[:, h0:10, :, 1, :]
                c1 = dve.op([b8, S["qkT_rd"][b]], nc.vector.tensor_tensor, out=rt[0][:, 0:nh], in0=x1, in1=cosb, op=ALU.mult)
                c2 = dve.op([c1], nc.vector.tensor_tensor, out=rt[1][:, 0:nh], in0=x2, in1=sinb, op=ALU.mult)
                c3 = dve.op([c2], nc.vector.tensor_tensor, out=O[:, h0:10, :, 0, :], in0=rt[0][:, 0:nh], in1=rt[1][:, 0:nh], op=ALU.subtract)
                c4 = pool.op([b8, S["qkT_rd"][b]], nc.gpsimd.tensor_tensor, out=rt[2][:, 0:nh], in0=x2, in1=cosb, op=ALU.mult)
                c5 = pool.op([c4], nc.gpsimd.tensor_tensor, out=rt[3][:, 0:nh], in0=x1, in1=sinb, op=ALU.mult)
                c6 = pool.op([c5], nc.gpsimd.tensor_tensor, out=O[:, h0:10, :, 1, :], in0=rt[2][:, 0:nh], in1=rt[3][:, 0:nh], op=ALU.add)
                S["qk_ready"][t] = [c3, c6]
                S[("sq_rd", t % 2)] = [c3, c6]
                S["tv"][t] = tv

            t6s = {}

            def qtr(t):
                b = t % 2
                tt = t % 4
                t5 = None
                for kc in range(8):
                    t5 = pe.op(S["qk_ready"][t] + [S["pst_rd"]], nc.tensor.transpose, ps_t[:, kc, :],
                               qkbf[b][:, kc * 128:(kc + 1) * 128], ident[:])
                t6 = act.op([t5, S["st_out"][0] if tt == 0 else None], nc.scalar.copy,
                            out=qst[:, :, tt * 128:(tt + 1) * 128], in_=ps_t[:])
                S["pst_rd"] = t6
                t6s[t] = t6

            def ktr(t):
                b = t % 2
                g4, tt = divmod(t, 4)
                t7 = None
                for kc in range(nkc):
                    t7 = pe.op(S["qk_ready"][t] + [S["pst_rd"]], nc.tensor.transpose, ps_t[:, kc, :],
                               qkbf[b][:, 1024 + kc * 128:1024 + (kc + 1) * 128], ident[:])
                S["qkT_rd"][b] = t7
                t8 = act.op([t7, S["st_out"][1] if tt == 0 else None], nc.scalar.copy,
                            out=kst[:, :, tt * 128:(tt + 1) * 128], in_=ps_t[:, 0:nkc, :])
                S["pst_rd"] = t8
                if tt == 3:
                    if need_q:
                        S["st_out"][0] = dma(sp, s_o[0], [t6s[t]], qd[:, :, g4 * 512:(g4 + 1) * 512].rearrange("c p t -> p c t"), qst[:])
                    kdst = kvo[0:nkc * 128, g4 * 512:(g4 + 1) * 512].rearrange("(c p) t -> p c t", p=128)
                    S["st_out"][1] = dma(sp, s_o[1], [t8], kdst, kst[:])
                    vdst = kvo[nkc * 128:2 * nkc * 128, g4 * 4 * VW:(g4 + 1) * 4 * VW].rearrange("(c p) (t e) -> p c t e", p=128, e=VW)
                    st_out_v[g4 % 2] = dma(sp, s_ov[g4 % 2], [S["tv"][t], t_ones_v], vdst, vst2[g4 % 2][:])

            ntl = DBG["nt"]
            norm(0)
            htr(0)
            for t in range(ntl):
                nxt = t + 1 < ntl
                prev = t > 0
                if nxt:
                    norm(t + 1)
                if layer == 0:
                    mm(t, 0)
                    mm(t, 1)
                    post_qk(t, 0)
                    if nxt:
                        htr(t + 1)
                    mm(t, 2)
                    mm(t, 3)
                    post_qk(t, 1)
                    if prev:
                        qtr(t - 1)
                    mm(t, 4)
                    mm(t, 5)
                    post_v0(t)
                    if prev:
                        ktr(t - 1)
                elif need_q:
                    mm(t, 0)
                    if nxt:
                        htr(t + 1)
                    mm(t, 1)
                    if prev:
                        qtr(t - 1)
                    mm(t, 2)
                    post1(t)
                    if prev:
                        ktr(t - 1)
                else:
                    mm(t, 2)
                    if nxt:
                        htr(t + 1)
                    post1(t)
                    if prev:
                        ktr(t - 1)
            if need_q:
                qtr(ntl - 1)
            ktr(ntl - 1)
            barrier()

    def compute_lam(st):
        lp = sb("lamp_sb", [128, 4, 64], F32, st)
        lpr = sb("lam_pr", [128, 2, 64], F32, st)
        gsub = sb("gsub", [128, 128], F32, st)
        t0 = dma(sp, s_c, [], lp[:].rearrange("p a b -> p (a b)"), lamp.ap.rearrange("a b -> (a b)").partition_broadcast(128))
        t1 = dma(sp, s_c, [], gsub[:], subln_g.ap.rearrange("a b -> (a b)").partition_broadcast(128))
        a = dve.op([t1], nc.vector.tensor_tensor, out=lpr[:], in0=lp[:, 0:2, :], in1=lp[:, 2:4, :], op=ALU.mult)
        a = dve.op([a], nc.vector.tensor_reduce, out=lam_t[:, 0:2], in_=lpr[:], axis=AX.X, op=ALU.add)
        a = act.op([a], nc.scalar.activation, out=lam_t[:, 0:2], in_=lam_t[:, 0:2], func=AF.Exp)
        a = dve.op([a], nc.vector.tensor_tensor, out=lam_t[:, 2:3], in0=lam_t[:, 0:1], in1=lam_t[:, 1:2], op=ALU.subtract)
        a = dve.op([a], nc.vector.tensor_scalar, out=lam_t[:, 3:4], in0=lam_t[:, 2:3], scalar1=lambda_init_fn(0), scalar2=-1.0,
                   op0=ALU.add, op1=ALU.mult)
        a = dve.op([a], nc.vector.tensor_scalar, out=gsub[:], in0=gsub[:], scalar1=1.0 - lambda_init_fn(0), scalar2=None,
                   op0=ALU.mult)
        return gsub, a

    def stage_CD(layer, qd=None):
        nunits = 8 if layer == 0 else 4
        kva = kva0 if layer == 0 else kva1
        if qd is None:
            qd = q0 if layer == 0 else q1
        nkc = 8 if layer == 0 else 2
        rows = 2 * nkc * 128
        sc = (64 ** -0.5) if layer == 0 else (128 ** -0.5)
        wo_d = wo0 if layer == 0 else wo1
        barrier()
        with ExitStack() as st0:
            o_tm = sb("c_o", [128, NT, D], BF16, st0)
            wo_sb = sb("wo_sb", [128, 8, D], BF16, st0)
            for kc in range(0, 8, 2):
                dma(pool, s_w[3], [], wo_sb[:, kc:kc + 2, :], wo_d[:, kc:kc + 2, :], max_dma_last_dim=4096)
            t_wo = s_w[3].last()
            if DBG.get("units", 99) < 8:
                dve.op([], nc.vector.memset, o_tm[:], 0.0)
            with ExitStack() as st:
                gsub, t_lam = (None, None)
                if layer == 0:
                    gsub, t_lam = compute_lam(st)
                Qb = [sb("c_q%d" % i, [128, 2, TOWN], BF16, st) for i in range(2)]
                t_qz = None
                if layer == 0:
                    for i in range(2):
                        dve.op([], nc.vector.memset, Qb[i][64:128, 0, :], 0.0)
                        t_qz = dve.op([], nc.vector.memset, Qb[i][0:64, 1, :], 0.0)
                Kb = [sb("c_k%d" % i, [128, SEQ], BF16, st) for i in range(2)]
                Vb = [sb("c_v%d" % i, [128, 32, VW], BF16, st) for i in range(2)]
                pt = [sb("c_pt%d" % i, [128, 2, 512], BF16, st) for i in range(3)]
                osb = [sb("c_osb%d" % i, [128, 8, VW], F32, st) for i in range(2)]
                if layer == 0:
                    ubuf = [sb("c_u%d" % i, [128, 4, 128], F32, st) for i in range(2)]
                    fjunk = sb("c_fj", [128, 128], F32, st)
                rec = [sb("c_rec%d" % i, [128, 8], F32, st) for i in range(2)]
                ssq4 = [sb("c_ssq%d" % i, [128, 4], F32, st) for i in range(2)]
                ps_s = ps("ps_s", [128, 2, 2, 512], F32, st)
                ps_o = [ps("ps_o%d" % i, [128, 512], F32, st) for i in range(3)]
                s_ld = [Slot(ctx, "ld%d_%d" % (layer, i)) for i in range(2)]

                units = list(range(nunits))[:DBG.get("units", 99)]
                qbs = list(range(4))[:DBG.get("qbs", 99)]
                kts = list(range(32))[:DBG.get("kts", 99)]
                steps = [(u, qb, kt) for u in units for qb in qbs for kt in kts]
                t_unit_done = {}
                t_loaded = {}

                def load_unit(u):
                    b = u % 2
                    deps = [t_unit_done.get(u - 2)]
                    sl = s_ld[b]
                    if layer == 0:
                        dma(sp, sl, deps, Qb[b][0:64, 0, :], qd[u, 0:64, :])
                        dma(sp, sl, deps, Qb[b][64:128, 1, :], qd[u, 64:128, :])
                        kc = u
                    else:
                        dma(sp, sl, deps, Qb[b][:], qd[2 * u:2 * u + 2].rearrange("c p t -> p c t"))
                        kc = u // 2
                    for r in range(2):
                        dma(sp, sl, deps, Kb[b][:, r * TOWN:(r + 1) * TOWN],
                            kva[r * rows + kc * 128:r * rows + (kc + 1) * 128, 0:TOWN])
                        dma(sp, sl, deps, Vb[b][:, r * 16:(r + 1) * 16, :],
                            kva[r * rows + (nkc + kc) * 128:r * rows + (nkc + kc + 1) * 128, :].rearrange("p (t e) -> p t e", e=VW))
                    t_loaded[u] = sl.last()

                acc_loc = []
                for a in range(8):
                    acc_loc.append((a // 3, (a % 3) * VW))

                t_S = {}
                t_exp = {}
                t_pv = {}
                t_evac = {}
                t_fin_rd = {}
                deferred = []

                def emit_S(i):
                    u, qb, kt = steps[i]
                    b = u % 2
                    sbuf_i = i % 2
                    deps = [t_loaded[u], t_exp.get(i - 2), t_qz]
                    tk = None
                    for lane in range(2):
                        lhsT = Kb[b][:, kt * 128:(kt + 1) * 128]
                        rhs = Qb[b][:, lane, qb * 512:(qb + 1) * 512]
                        tk = (pe.op if lane == 1 else pe.op_nomark)(deps, nc.tensor.matmul, ps_s[:, sbuf_i, lane, :], lhsT=lhsT, rhs=rhs,
                                                                    start=True, stop=True)
                    t_S[i] = tk

                def emit_exp(i):
                    deps = [t_S[i], t_pv.get(i - 3)]
                    t_exp[i] = act.op(deps, nc.scalar.activation, out=pt[i % 3][:].rearrange("p a b -> p (a b)"),
                                      in_=ps_s[:, i % 2].rearrange("p a b -> p (a b)"), func=AF.Exp, scale=sc, chain=False)

                def emit_PV(i, blk):
                    u, qb, kt = steps[i]
                    b = u % 2
                    tk = None
                    for lane in range(2):
                        for qs in range(4):
                            a = lane * 4 + qs
                            bank, off = acc_loc[a]
                            first_in_bank = (a % 3 == 0)
                            deps = [t_exp[i]]
                            if kt == kts[0] and (blk - 1) in t_evac:
                                deps.append(t_evac[blk - 1][bank])
                            tk = (pe.op if a == 7 else pe.op_nomark)(deps, nc.tensor.matmul, ps_o[bank][:, off:off + VW],
                                       lhsT=pt[i % 3][:, lane, qs * 128:(qs + 1) * 128], rhs=Vb[b][:, kt, :],
                                       start=(kt == kts[0] and first_in_bank), stop=(kt == kts[-1]), skip_group_check=True)
                    t_pv[i] = tk

                def emit_evac(i, blk):
                    bb = blk % 2
                    deps = [t_pv[i], t_fin_rd.get(blk - 2)]
                    ta = dve.op(deps, nc.vector.tensor_copy, out=osb[bb][:, 0:3, :].rearrange("p a b -> p (a b)"), in_=ps_o[0][:, 0:3 * VW])
                    tb_ = dve.op([ta], nc.vector.tensor_copy, out=osb[bb][:, 3:6, :].rearrange("p a b -> p (a b)"), in_=ps_o[1][:, 0:3 * VW])
                    tc = dve.op([tb_], nc.vector.tensor_copy, out=osb[bb][:, 6:8, :].rearrange("p a b -> p (a b)"), in_=ps_o[2][:, 0:2 * VW])
                    t_evac[blk] = [ta, tb_, tc]

                def finalize(i, blk):
                    u, qb, kt = steps[i]
                    bb = blk % 2
                    O = osb[bb]
                    t = dve.op([t_evac[blk][2]], nc.vector.reciprocal, out=rec[bb][:], in_=O[:, :, 128])
                    if layer == 1:
                        for lane in range(2):
                            for qs in range(4):
                                a = lane * 4 + qs
                                head = 2 * u + lane
                                t = dve.op([t], nc.vector.tensor_scalar, out=o_tm[:, qb * 4 + qs, head * 128:(head + 1) * 128],
                                           in0=O[:, a, 0:128], scalar1=rec[bb][:, a:a + 1], scalar2=None, op0=ALU.mult)
                        t_fin_rd[blk] = t
                        return
                    t = dve.op([t, t_lam], nc.vector.tensor_scalar, out=rec[bb][:, 4:8], in0=rec[bb][:, 4:8], scalar1=lam_t[:, 3:4],
                               scalar2=None, op0=ALU.mult)
                    for qs in range(4):
                        t = dve.op([t], nc.vector.tensor_scalar, out=ubuf[bb][:, qs, :], in0=O[:, qs, 0:128],
                                   scalar1=rec[bb][:, qs:qs + 1], scalar2=None, op0=ALU.mult)
                        t = dve.op([t], nc.vector.scalar_tensor_tensor, out=ubuf[bb][:, qs, :], in0=O[:, 4 + qs, 0:128],
                                   scalar=rec[bb][:, 4 + qs:5 + qs], in1=ubuf[bb][:, qs, :], op0=ALU.mult, op1=ALU.add)
                        t = dve.op([t], nc.vector.scalar_tensor_tensor, out=fjunk[:], in0=ubuf[bb][:, qs, :], scalar=1.0,
                                   in1=ubuf[bb][:, qs, :], op0=ALU.mult, op1=ALU.mult, accum_out=ssq4[bb][:, qs:qs + 1])
                    t = dve.op([t], nc.vector.tensor_scalar, out=ssq4[bb][:], in0=ssq4[bb][:], scalar1=1.0 / 128, scalar2=EPS,
                               op0=ALU.mult, op1=ALU.add)
                    t_fin_rd[blk] = t
                    state = {"t": t}

                    def part_act():
                        a = act.op([state["t"]], nc.scalar.activation, out=ssq4[bb][:], in_=ssq4[bb][:], func=AF.Ln)
                        state["t"] = act.op([a], nc.scalar.activation, out=ssq4[bb][:], in_=ssq4[bb][:], func=AF.Exp, scale=-0.5)

                    def part_dve():
                        t2 = state["t"]
                        for qs in range(4):
                            t2 = dve.op([t2], nc.vector.scalar_tensor_tensor, out=o_tm[:, qb * 4 + qs, u * 128:(u + 1) * 128],
                                        in0=ubuf[bb][:, qs, :], scalar=ssq4[bb][:, qs:qs + 1], in1=gsub[:], op0=ALU.mult, op1=ALU.mult)
                        t_fin_rd[blk] = t2

                    deferred.append((i + (4 if len(kts) >= 16 else 1), part_act))
                    deferred.append((i + (7 if len(kts) >= 16 else 1), part_dve))

                load_unit(units[0])
                if len(units) > 1:
                    load_unit(units[1])
                nsteps = len(steps)
                emit_S(0)
                if nsteps > 1:
                    emit_S(1)
                blk = 0
                for i in range(nsteps):
                    u, qb, kt = steps[i]
                    emit_exp(i)
                    if i + 2 < nsteps:
                        emit_S(i + 2)
                    emit_PV(i, blk)
                    while deferred and deferred[0][0] <= i:
                        deferred.pop(0)[1]()
                    if kt == kts[-1]:
                        emit_evac(i, blk)
                        finalize(i, blk)
                        blk += 1
                        if qb == qbs[-1]:
                            t_unit_done[u] = t_pv[i]
                            ui = units.index(u)
                            if ui + 2 < len(units):
                                load_unit(units[ui + 2])
                while deferred:
                    deferred.pop(0)[1]()
                barrier()
            with ExitStack() as st:
                oT = [sb("d_oT%d" % i, [128, 8, 128], BF16, st) for i in range(2)]
                tmpd = [sb("d_tmp%d" % i, [128, D], F32, st) for i in range(2)]
                ps_tr = [ps("ps_dtr%d" % i, [128, 8, 128], BF16, st) for i in range(2)]
                ps_y = [ps("ps_dy%d" % i, [128, 2, 512], F32, st) for i in range(2)]
                gate = mod[:, 0, 2, :]
                t_cp = [None, None]
                t_mmd = [None, None]
                t_ep = [None, None]
                pend_ssq = []
                for t in range(DBG["nt"]):
                    b = t % 2
                    tt = None
                    for kc in range(8):
                        tt = pe.op([t_cp[b]], nc.tensor.transpose, ps_tr[b][:, kc, :], o_tm[:, t, kc * 128:(kc + 1) * 128], ident[:])
                    t_cp[b] = act.op([tt, t_mmd[b]], nc.scalar.copy, out=oT[b][:], in_=ps_tr[b][:])
                    tm = None
                    for cb in range(2):
                        for kc in range(8):
                            tm = pe.op([t_cp[b], t_wo, t_ep[b]], nc.tensor.matmul, ps_y[b][:, cb, :], lhsT=oT[b][:, kc, :],
                                       rhs=wo_sb[:, kc, cb * 512:(cb + 1) * 512], start=(kc == 0), stop=(kc == 7))
                    t_mmd[b] = tm
                    e1 = dve.op([tm], nc.vector.tensor_tensor, out=tmpd[b][:], in0=ps_y[b][:].rearrange("p a b -> p (a b)"),
                                in1=gate, op=ALU.mult)
                    t_ep[b] = e1
                    e2 = dve.op([e1], nc.vector.tensor_tensor, out=x_sb[:, t, :], in0=x_sb[:, t, :], in1=tmpd[b][:], op=ALU.add)
                    pend_ssq.append((e2, t))
                    if len(pend_ssq) > 1:
                        t_ssq_last = tile_sumsq(*pend_ssq.pop(0))
                while pend_ssq:
                    t_ssq_last = tile_sumsq(*pend_ssq.pop(0))
                if DBG["nt"] == NT:
                    xstate["rstd_tok"] = norm_stats(have_ss=t_ssq_last)
                barrier()

    def stage_E(layer):
        barrier()
        with ExitStack() as st:
            wo_sb = sb("e_wo", [128, NJ, D], BF16, st)
            hTb = sb("e_hT", [128, 8, 512], BF16, st)
            aT = sb("e_aT", [128, NJ, 512], BF16, st)
            wi = [sb("e_wi%d" % i, [128, 8, 2, 256], BF16, st) for i in range(2)]
            sg = [sb("e_sg%d" % i, [128, 512], F32, st) for i in range(2)]
            tmp = [sb("e_tmp%d" % i, [128, D], F32, st) for i in range(2)]
            hbf = [sb("e_h%d" % i, [128, D], BF16, st) for i in range(2)]
            ps_tr = ps("ps_etr", [128, 8, 128], BF16, st)
            ps_gu = ps("ps_gu", [128, 2, 2, 512], F32, st)
            ps_y = ps("ps_ey", [128, 3, 512], F32, st)
            for j in range(0, NJ, 2):
                dma(pool, s_w[3], [], wo_sb[:, j:j + 2, :], wout[layer, :, j:j + 2, :], max_dma_last_dim=4096)
            t_wo = s_w[3].last()
            t_rstd = norm_stats()
            mslot = mod[:, 1]
            gate = mslot[:, 2, :]
            t_wi_rd = [None, None]
            t_wback = [None, None]
            t_gu_rd = [None, None]
            t_y_rd = [None, None, None]
            t_hT_rd = None
            t_aT_rd = None
            t_tmp = [None, None]
            t_hbf_rd = [None, None]
            t_tr_rd = None
            gi = 0
            ji = 0
            yi = 0
            nblk = DBG["nt"] // 4
            for tb in range(nblk):
                t_hT = None
                for tt in range(4):
                    t = tb * 4 + tt
                    b = t % 2
                    t1 = dve.op([t_rstd, t_tmp[b]], nc.vector.scalar_tensor_tensor, out=tmp[b][:], in0=x_sb[:, t, :],
                                scalar=rstd[:, t:t + 1], in1=mslot[:, 1, :], op0=ALU.mult, op1=ALU.mult)
                    t2 = dve.op([t1, t_hbf_rd[b]], nc.vector.tensor_tensor, out=hbf[b][:], in0=tmp[b][:], in1=mslot[:, 0, :], op=ALU.add)
                    t_tmp[b] = t2
                    t3 = None
                    for kc in range(8):
                        t3 = pe.op([t2, t_tr_rd], nc.tensor.transpose, ps_tr[:, kc, :], hbf[b][:, kc * 128:(kc + 1) * 128], ident[:])
                    t_hbf_rd[b] = t3
                    t_hT = act.op([t3, t_hT_rd], nc.scalar.copy, out=hTb[:, :, tt * 128:(tt + 1) * 128], in_=ps_tr[:])
                    t_tr_rd = t_hT
                t_a_last = None
                for g in range(NG):
                    wb = gi % 2
                    wflat = wi[wb][:].rearrange("p a b c -> p (a b c)")
                    if tb == 0 and not win_cached[layer]:
                        t_w = dma(pool, s_w[wb], [t_wi_rd[wb], t_wback[wb]], wflat,
                                  win[layer, :, g].rearrange("p a b c -> p (a b c)"), max_dma_last_dim=4096)
                        t_wback[wb] = dma(sp, s_wb[wb], [t_w], winb[layer, :, g, :], wflat)
                    else:
                        deps = [t_wi_rd[wb], t_wback[wb]]
                        if tb == 1 and g == 0 and not win_cached[layer]:
                            deps += [s_wb[0].last(), s_wb[1].last()]
                        t_w = dma(sp, s_wl[wb], deps, wflat, winb[layer, :, g, :])
                    gi += 1
                    for jj in range(2):
                        j = 2 * g + jj
                        slot = ji % 2
                        ji += 1
                        tmm = [None, None]
                        for gu in range(2):
                            for kc in range(8):
                                tmm[gu] = pe.op([t_w, t_hT, t_gu_rd[slot], t_aT_rd], nc.tensor.matmul, ps_gu[:, slot, gu, :],
                                                lhsT=wi[wb][:, kc, gu, jj * 128:(jj + 1) * 128], rhs=hTb[:, kc, :],
                                                start=(kc == 0), stop=(kc == 7))
                        t_wi_rd[wb] = tmm[1]
                        ts = act.op([tmm[0]], nc.scalar.activation, out=sg[slot][:], in_=ps_gu[:, slot, 0, :], func=AF.Silu)
                        ta = dve.op([ts, tmm[1]], nc.vector.tensor_tensor, out=aT[:, j, :], in0=ps_gu[:, slot, 1, :], in1=sg[slot][:], op=ALU.mult)
                        t_gu_rd[slot] = ta
                        t_a_last = ta
                t_hT_rd = t_wi_rd[(gi - 1) % 2]
                t_last_out = None
                for tt in range(4):
                    t = tb * 4 + tt
                    for cb in range(2):
                        ys = yi % 3
                        yi += 1
                        tm = None
                        for j in range(NJ):
                            tm = pe.op([t_a_last, t_wo, t_y_rd[ys]], nc.tensor.matmul, ps_y[:, ys, :], lhsT=aT[:, j, tt * 128:(tt + 1) * 128],
                                       rhs=wo_sb[:, j, cb * 512:(cb + 1) * 512], start=(j == 0), stop=(j == NJ - 1))
                        t_last_out = tm
                        b = yi % 2
                        e1 = dve.op([tm], nc.vector.tensor_tensor, out=tmp[b][:, 0:512], in0=ps_y[:, ys, :],
                                    in1=gate[:, cb * 512:(cb + 1) * 512], op=ALU.mult)
                        t_y_rd[ys] = e1
                        t_tmp[b] = dve.op([e1], nc.vector.tensor_tensor, out=x_sb[:, t, cb * 512:(cb + 1) * 512],
                                          in0=x_sb[:, t, cb * 512:(cb + 1) * 512], in1=tmp[b][:, 0:512], op=ALU.add)
                        if cb == 1:
                            t_ssq_last = tile_sumsq(t_tmp[b], t)
                t_aT_rd = t_last_out
            win_cached[layer] = True
            if DBG["nt"] == NT:
                xstate["rstd_tok"] = norm_stats(have_ss=t_ssq_last)
            barrier()

    def final_norm():
        barrier()
        with ExitStack() as st:
            gB = sb("f_g", [128, D], F32, st)
            ob = [sb("f_o%d" % i, [128, D], F32, st) for i in range(2)]
            s_f = [Slot(ctx, "f%d" % i) for i in range(2)]
            t_g = dma(sp, s_c, [], gB[:], final_g.ap.rearrange("a b -> (a b)").partition_broadcast(128))
            t_rstd = norm_stats()
            outv = x_out.rearrange("(t p) d -> p t d", p=128)
            t_st = [None, None]
            for t in range(NT):
                b = t % 2
                t1 = dve.op([t_rstd, t_g, t_st[b]], nc.vector.scalar_tensor_tensor, out=ob[b][:], in0=x_sb[:, t, :],
                            scalar=rstd[:, t:t + 1], in1=gB[:], op0=ALU.mult, op1=ALU.mult)
                t_st[b] = dma(sp, s_f[b], [t1], outv[:, t, :], ob[b][:])
            barrier()

    def store_x():
        barrier()
        s_f = Slot(ctx, "xs")
        outv = x_out.rearrange("(t p) d -> p t d", p=128)
        for g in range(4):
            dma(sp, s_f, [], outv[:, 4 * g:4 * g + 4, :], x_sb[:, 4 * g:4 * g + 4, :])
        barrier()

    def adaln_all():
        for idx, (layer, half, gain) in enumerate(((0, 0, norm1_g[0]), (0, 1, norm2_g[0]), (1, 0, norm1_g[1]), (1, 1, norm2_g[1]))):
            adaln(layer, half, gain)
            dma(sp, s_m, [], modsave[idx:idx + 1, :], mod[0:1, half].rearrange("p a b -> p (a b)"))
        barrier()

    def load_mod(idx, half):
        load_mods([(idx, half)])

    def load_mods(pairs):
        barrier()
        for idx, half in pairs:
            dma(sp, s_m, [], mod[:, half].rearrange("p a b -> p (a b)"), modsave[idx].partition_broadcast(128))
        barrier()

    def exchange(layer):
        barrier()
        kvo = kvo0 if layer == 0 else kvo1
        kva = kva0 if layer == 0 else kva1
        s_cc = Slot(ctx, "cc%d" % layer)
        for d_ in []:
            pass
        pool.h.collective_compute("AllGather", ALU.bypass, replica_groups=[[0, 1], [2, 3], [4, 5], [6, 7]],
                                  ins=[kvo], outs=[kva]).then_inc(s_cc.sem, 16)
        s_cc.cnt += 16
        barrier()

    if fused:
        adaln_all()
        ada_stack.close()
        load_mod(0, 0)
        stage_A(0, qd=q0, kvo=kva0[2048:4096], rope=ropeA.ap)
        barrier()
        load_x(x_oth)
        stage_A(0, qd=q0b, kvo=kva0[0:2048], rope=ropeA_o.ap)
        stage_CD(0, qd=q0b)
        load_mods([(1, 1), (2, 0)])
        stage_E(0)
        stage_A(1, qd=q1, kvo=kva1[0:512], rope=ropeB_o.ap, need_q=False)
        barrier()
        load_x(x_in)
        load_mod(0, 0)
        stage_CD(0, qd=q0)
        load_mod(2, 0)
        stage_E(0)
        stage_A(1, qd=q1, kvo=kva1[512:1024], rope=ropeB.ap)
        stage_CD(1, qd=q1)
        load_mod(3, 1)
        stage_E(1)
        final_norm()
        phases = []
    for ph in phases:
        if ph == "A0":
            adaln(0, 0, norm1_g[0])
            stage_A(0)
        elif ph == "A1":
            adaln(1, 0, norm1_g[1])
            stage_A(1)
            if last == "A1":
                store_x()
        elif ph == "C0":
            if "A0" not in phases:
                adaln(0, 0, norm1_g[0])
            if "CD" not in DBG.get("skip", ()):
                stage_CD(0)
            adaln(0, 1, norm2_g[0])
            if "E" not in DBG.get("skip", ()):
                stage_E(0)
        elif ph == "C1":
            if "A1" not in phases:
                adaln(1, 0, norm1_g[1])
            if "CD" not in DBG.get("skip", ()):
                stage_CD(1)
            adaln(1, 1, norm2_g[1])
            if "E" not in DBG.get("skip", ()):
                stage_E(1)
            final_norm()
        elif ph == "DBG_ADA":
            adaln(0, 0, norm1_g[0])
            dbg = nc.dram_tensor("dbg", [128, 3 * D], F32, kind="ExternalOutput").ap()
            dma(sp, s_m, [], dbg, mod[:, 0].rearrange("p a b -> p (a b)"))
        elif ph == "DBG_INIT":
            barrier()
            dbg = nc.dram_tensor("dbg", [128, 8 * 128], BF16, kind="ExternalOutput").ap()
            dma(sp, s_m, [], dbg, condB[:].rearrange("p a b -> p (a b)"))
        else:
            raise NotImplementedError(ph)

    barrier()
    if not fused:
        ada_stack.close()
    es.close()
    return nc, used_inputs


def _bf16(a):
    return np.ascontiguousarray(a).astype(ml_dtypes.bfloat16)


def rope_tables():
    t = np.arange(SEQ, dtype=np.int32)
    rows = SEQ // 64
    row_pos = np.broadcast_to(np.arange(rows, dtype=np.int32)[:, None], (rows, 64)).reshape(SEQ)
    col_pos = np.broadcast_to(np.arange(64, dtype=np.int32)[None, :], (rows, 64)).reshape(SEQ)

    def ang(pos, dim, theta):
        half = dim // 2
        freqs = (np.float32(theta) ** (-np.arange(half, dtype=np.float32) / np.float32(half))).astype(np.float32)
        a = pos.astype(np.float32)[:, None] * freqs[None, :]
        return np.cos(a).astype(np.float32), np.sin(a).astype(np.float32)

    ca, sa = ang(t, 16, 500000.0)
    cr, sr = ang(row_pos, 64, 10000.0)
    cc, sc = ang(col_pos, 64, 10000.0)
    A = np.stack([ca, sa], axis=1)
    B = np.stack([cr, sr, cc, sc], axis=1)
    return A, B


def common_inputs(inp):
    f = lambda a: np.ascontiguousarray(np.asarray(a, dtype=np.float32))
    m = {}
    m["ident"] = np.eye(128, dtype=np.float32).astype(ml_dtypes.bfloat16)
    aw = f(inp["ada_w"]).reshape(2, 8, 128, 6, 1024).transpose(0, 2, 3, 1, 4)
    m["ada_w"] = np.ascontiguousarray(aw)
    m["ada_b"] = f(inp["ada_b"])
    m["norm1_g"] = f(inp["norm1_g"])
    m["norm2_g"] = f(inp["norm2_g"])
    m["wqkv0"] = np.ascontiguousarray(f(inp["a_w_qkv"])[0].reshape(8, 128, 3072).transpose(1, 0, 2))
    m["wo0"] = np.ascontiguousarray(f(inp["a_w_o"])[0].reshape(8, 128, 1024).transpose(1, 0, 2))
    m["lamp"] = np.ascontiguousarray(np.concatenate([f(inp["a_lam_q1"]), f(inp["a_lam_q2"]), f(inp["a_lam_k1"]), f(inp["a_lam_k2"])], axis=0))
    m["subln_g"] = f(inp["a_subln_g"])
    m["wqkv1"] = np.ascontiguousarray(f(inp["b_w_qkv"])[0].reshape(8, 128, 1536).transpose(1, 0, 2))
    m["wo1"] = np.ascontiguousarray(f(inp["b_w_o"])[0].reshape(8, 128, 1024).transpose(1, 0, 2))
    m["qk_g"] = np.ascontiguousarray(np.concatenate([f(inp["b_qnorm_g"]), f(inp["b_knorm_g"])], axis=0))
    wi = f(inp["f_w_in"]).reshape(2, 8, 128, 2, NG, 256).transpose(0, 2, 4, 1, 3, 5)
    m["win"] = np.ascontiguousarray(wi)
    wo = f(inp["f_w_out"]).reshape(2, NJ, 128, 1024).transpose(0, 2, 1, 3)
    m["wout"] = np.ascontiguousarray(wo)
    m["final_g"] = f(inp["final_g"]).reshape(1, D)
    return m


def core_inputs(inp, core, A, B):
    b, r = divmod(core, 2)
    x = np.asarray(inp["x"], dtype=np.float32)
    c = np.asarray(inp["c"], dtype=np.float32)
    m = {}
    m["x_in"] = np.ascontiguousarray(x[b, r * TOWN:(r + 1) * TOWN])
    m["cT"] = np.ascontiguousarray(c[b].reshape(8, 128).T)
    sl = slice(r * TOWN, (r + 1) * TOWN)
    m["ropeA"] = np.ascontiguousarray(A[sl].reshape(NT, 128, 2, 8).transpose(1, 0, 2, 3))
    m["ropeB"] = np.ascontiguousarray(B[sl].reshape(NT, 128, 4, 32).transpose(1, 0, 2, 3))
    return m


_PROG_CACHE = {}


def _get_prog(phases, fused):
    key = (tuple(phases), fused)
    if key not in _PROG_CACHE:
        _PROG_CACHE[key] = build(list(phases), fused)
    return _PROG_CACHE[key]


def _run(phases, fused, per_core):
    import time as _t
    t0 = _t.time()
    nc, used = _get_prog(phases, fused)
    in_maps = [{k: m[k] for k in used} for m in per_core]
    t1 = _t.time()
    res = run_bass_kernel_spmd(nc, in_maps, core_ids=list(range(NCORES)))
    print("[kernel] launch %s build %.1fs run %.1fs" % (phases, t1 - t0, _t.time() - t1), flush=True)
    return res.results


def kernel_unfused(**inp):
    A, B = rope_tables()
    cm = common_inputs(inp)
    per_core = []
    for core in range(NCORES):
        m = dict(cm)
        m.update(core_inputs(inp, core, A, B))
        per_core.append(m)
    r1 = _run(["A0"], False, per_core)
    for core in range(NCORES):
        b = core // 2
        per_core[core]["q0"] = r1[core]["q0"]
        per_core[core]["kva0"] = np.concatenate([r1[2 * b]["kvo0"], r1[2 * b + 1]["kvo0"]], axis=0)
    r2 = _run(["C0", "A1"], False, per_core)
    for core in range(NCORES):
        b = core // 2
        per_core[core]["q1"] = r2[core]["q1"]
        per_core[core]["kva1"] = np.concatenate([r2[2 * b]["kvo1"], r2[2 * b + 1]["kvo1"]], axis=0)
        per_core[core]["x_in"] = r2[core]["x_mid"]
    r3 = _run(["C1"], False, per_core)
    out = np.empty((NB, SEQ, D), dtype=np.float32)
    for core in range(NCORES):
        b, r = divmod(core, 2)
        out[b, r * TOWN:(r + 1) * TOWN] = np.asarray(r3[core]["out"], dtype=np.float32)
    return out


def kernel_fused(**inp):
    A, B = rope_tables()
    cm = common_inputs(inp)
    x = np.asarray(inp["x"], dtype=np.float32)
    per_core = []
    for core in range(NCORES):
        b, r = divmod(core, 2)
        m = dict(cm)
        m.update(core_inputs(inp, core, A, B))
        o = 1 - r
        sl = slice(o * TOWN, (o + 1) * TOWN)
        m["x_oth"] = np.ascontiguousarray(x[b, sl])
        m["ropeA_o"] = np.ascontiguousarray(A[sl].reshape(NT, 128, 2, 8).transpose(1, 0, 2, 3))
        m["ropeB_o"] = np.ascontiguousarray(B[sl].reshape(NT, 128, 4, 32).transpose(1, 0, 2, 3))
        per_core.append(m)
    r = _run(["A0", "C0", "A1", "C1"], True, per_core)
    out = np.empty((NB, SEQ, D), dtype=np.float32)
    for core in range(NCORES):
        b, rr = divmod(core, 2)
        out[b, rr * TOWN:(rr + 1) * TOWN] = np.asarray(r[core]["out"], dtype=np.float32)
    return out


def kernel(**inp):
    return kernel_fused(**inp)
```
